# Optimizing a Trainium2 kernel written in Bass

```python
import math
import jax, jax.numpy as jnp
from jax import lax
import numpy as np

D_MODEL = 1024
BATCH = 8
SEQ = 8192
DEPTH = 2
DEC_BATCH = 16
DEC_SEQ = 32
PAST_LEN = 4096

CHUNK = 64
Q_BLOCK = 128
ROPE_THETA = 500000.0
N_A = (DEPTH + 1) // 2
N_C = DEPTH // 2
H_MLA = 8
MLA_NOPE = 64
MLA_ROPE = 32
MLA_V = 64
Q_LORA = 512
KV_LORA = 256
H_FOX = 8
FOX_DH = 64
FORGET_BIAS_LO = 2.0
FORGET_BIAS_HI = 6.0
H_DIFF = 8
DIFF_DH = 64
DIFF_ROT = DIFF_DH // 4
C_W = H_DIFF * 2 * DIFF_DH
D_FF = 4 * D_MODEL
FOX_W = H_FOX * FOX_DH
A_IN = Q_LORA + KV_LORA + MLA_ROPE + 3 * FOX_W + H_FOX
A_OUT = H_MLA * MLA_V + FOX_W

kernel_name = 'chunk_causal_mla_fox_diffattn_step'


def _rms(x, g, eps=1e-6):
    xf = x.astype(jnp.float32)
    y = xf * lax.rsqrt(jnp.mean(xf * xf, axis=-1, keepdims=True) + eps)
    return (y * g.astype(jnp.float32)).astype(x.dtype)


def _rope(x, pos, rot):
    half = rot // 2
    inv = ROPE_THETA ** (-jnp.arange(half, dtype=jnp.float32) / half)
    ang = pos.astype(jnp.float32)[:, None] * inv[None, :]
    cos = jnp.cos(ang)[None, :, None, :]
    sin = jnp.sin(ang)[None, :, None, :]
    xr = x[..., :rot].astype(jnp.float32)
    x1, x2 = xr[..., :half], xr[..., half:]
    r = jnp.concatenate([x1 * cos - x2 * sin, x2 * cos + x1 * sin], axis=-1).astype(x.dtype)
    return jnp.concatenate([r, x[..., rot:]], axis=-1)


def _mask(q_pos, k_pos, frame_causal):
    if frame_causal:
        return k_pos[None, :] <= q_pos[:, None]
    return (k_pos[None, :] // CHUNK) <= (q_pos[:, None] // CHUNK)


def _probs(q, k, q_pos, k_pos, frame_causal, q_cum=None, k_cum=None):
    s = jnp.einsum('bqhd,bkhd->bhqk', q, k, preferred_element_type=jnp.float32) * (q.shape[-1] ** -0.5)
    if q_cum is not None:
        s = s + jnp.swapaxes(q_cum, 1, 2)[..., :, None] - jnp.swapaxes(k_cum, 1, 2)[..., None, :]
    s = jnp.where(_mask(q_pos, k_pos, frame_causal)[None, None], s, -jnp.inf)
    return jax.nn.softmax(s, axis=-1)


def _apply(p, v):
    return jnp.einsum('bhqk,bkhe->bqhe', p, v.astype(jnp.float32)).astype(v.dtype)


def _sweep(fn, q_args, q_pos):
    nb = q_pos.shape[0] // Q_BLOCK

    def split(a):
        return jnp.moveaxis(a.reshape(a.shape[0], nb, Q_BLOCK, *a.shape[2:]), 1, 0)

    xs = tuple(split(a) for a in q_args) + (q_pos.reshape(nb, Q_BLOCK),)
    out = lax.map(lambda blk: fn(*blk), xs)
    out = jnp.moveaxis(out, 0, 1)
    return out.reshape(out.shape[0], nb * Q_BLOCK, *out.shape[3:])


def _a_project(h, pos, w_in, g_q, w_uq, g_kv, b_f):
    B, T, _ = h.shape
    proj = h @ w_in
    o1 = Q_LORA
    o2 = o1 + KV_LORA
    o3 = o2 + MLA_ROPE
    o4 = o3 + FOX_W
    o5 = o4 + FOX_W
    o6 = o5 + FOX_W
    q = (_rms(proj[..., :o1], g_q) @ w_uq).reshape(B, T, H_MLA, MLA_NOPE + MLA_ROPE)
    q = jnp.concatenate([q[..., :MLA_NOPE], _rope(q[..., MLA_NOPE:], pos, MLA_ROPE)], axis=-1)
    ckv = _rms(proj[..., o1:o2], g_kv)
    krope = _rope(proj[..., o2:o3][:, :, None, :], pos, MLA_ROPE)[:, :, 0, :]
    fq = proj[..., o3:o4].reshape(B, T, H_FOX, FOX_DH)
    fk = proj[..., o4:o5].reshape(B, T, H_FOX, FOX_DH)
    fv = proj[..., o5:o6].reshape(B, T, H_FOX, FOX_DH)
    logf = jax.nn.log_sigmoid((proj[..., o6:] + b_f).astype(jnp.float32))
    return q, fq, (ckv, krope, fk, fv, logf)


def _a_attend(qm, fq, cum_q, q_pos, km, vm, fk, fv, cum_k, k_pos):
    B, Tq = qm.shape[:2]
    om = _apply(_probs(qm, km, q_pos, k_pos, False), vm)
    of = _apply(_probs(fq, fk, q_pos, k_pos, True, cum_q, cum_k), fv)
    return jnp.concatenate([om.reshape(B, Tq, -1), of.reshape(B, Tq, -1)], axis=-1)


def _mixer_a(h, q_pos, past, w_in, g_q, w_uq, g_kv, w_ukv, b_f, w_out):
    T = h.shape[1]
    qm, fq, new = _a_project(h, q_pos, w_in, g_q, w_uq, g_kv, b_f)
    rows = new if past is None else tuple(
        jnp.concatenate([p.astype(n.dtype), n], axis=1) for p, n in zip(past, new))
    ckv, krope, fk, fv, logf = rows
    B, L = ckv.shape[:2]
    k_pos = jnp.arange(L)
    kv = (ckv @ w_ukv).reshape(B, L, H_MLA, MLA_NOPE + MLA_V)
    km = jnp.concatenate(
        [kv[..., :MLA_NOPE], jnp.broadcast_to(krope[:, :, None, :], (B, L, H_MLA, MLA_ROPE))], axis=-1)
    vm = kv[..., MLA_NOPE:]
    cum = jnp.cumsum(logf.astype(jnp.float32), axis=1)
    cum_q = cum[:, L - T:]
    attend = lambda a, b, c, p: _a_attend(a, b, c, p, km, vm, fk, fv, cum, k_pos)
    if past is None:
        o = _sweep(attend, (qm, fq, cum_q), q_pos)
    else:
        o = attend(qm, fq, cum_q, q_pos)
    return o @ w_out, new


def _c_project(h, pos, w_in):
    B, T, _ = h.shape
    proj = h @ w_in

    def heads(a):
        a = a.reshape(B, T, 2 * H_DIFF, DIFF_DH)
        return _rope(a, pos, DIFF_ROT).reshape(B, T, H_DIFF, 2 * DIFF_DH)

    q = heads(proj[..., :C_W])
    k = heads(proj[..., C_W:2 * C_W])
    v = proj[..., 2 * C_W:].reshape(B, T, H_DIFF, 2 * DIFF_DH)
    return q, k, v


def _c_attend(q, q_pos, k, v, k_pos, lam, g_sub, lam_init):
    B, Tq = q.shape[:2]
    p1 = _probs(q[..., :DIFF_DH], k[..., :DIFF_DH], q_pos, k_pos, False)
    p2 = _probs(q[..., DIFF_DH:], k[..., DIFF_DH:], q_pos, k_pos, False)
    o = jnp.einsum('bhqk,bkhe->bqhe', p1 - lam * p2, v.astype(jnp.float32))
    o = _rms(o, g_sub, 1e-5) * (1.0 - lam_init)
    return o.reshape(B, Tq, C_W).astype(v.dtype)


def _mixer_c(h, q_pos, past, w_in, lam_p, g_sub, w_out, lam_init):
    q, k, v = _c_project(h, q_pos, w_in)
    new = (k, v)
    rows = new if past is None else tuple(
        jnp.concatenate([p.astype(n.dtype), n], axis=1) for p, n in zip(past, new))
    ka, va = rows
    k_pos = jnp.arange(ka.shape[1])
    lp = lam_p.astype(jnp.float32)
    lam = jnp.exp(jnp.sum(lp[0] * lp[1])) - jnp.exp(jnp.sum(lp[2] * lp[3])) + lam_init
    attend = lambda a, p: _c_attend(a, p, ka, va, k_pos, lam, g_sub, lam_init)
    if past is None:
        o = _sweep(attend, (q,), q_pos)
    else:
        o = attend(q, q_pos)
    return o @ w_out, new


def _mlp(h, w_up, w_down):
    return jnp.square(jax.nn.relu(h @ w_up)) @ w_down


def _stack(rows, j):
    return jnp.stack([r[j] for r in rows])


def setup_inputs(seed: int = 0) -> dict:
    key = jax.random.key(seed)
    ks = iter(jax.random.split(key, 40))

    def nrm(shape, scale=1.0):
        return jax.random.normal(next(ks), shape, jnp.float32) * scale

    def gain(shape):
        return 1.0 + nrm(shape, 0.01)

    forget_bias = jnp.linspace(FORGET_BIAS_LO, FORGET_BIAS_HI, H_FOX, dtype=jnp.float32)
    return {
        'x_prompt': nrm((BATCH, SEQ, D_MODEL)),
        'x_sample': nrm((DEC_BATCH, DEC_SEQ, D_MODEL)),
        'cache_mla_ckv': nrm((N_A, DEC_BATCH, PAST_LEN, KV_LORA)),
        'cache_mla_krope': nrm((N_A, DEC_BATCH, PAST_LEN, MLA_ROPE)),
        'cache_fox_k': nrm((N_A, DEC_BATCH, PAST_LEN, H_FOX, FOX_DH)),
        'cache_fox_v': nrm((N_A, DEC_BATCH, PAST_LEN, H_FOX, FOX_DH)),
        'cache_fox_logf': jax.nn.log_sigmoid(nrm((N_A, DEC_BATCH, PAST_LEN, H_FOX)) + forget_bias),
        'cache_diff_k': nrm((N_C, DEC_BATCH, PAST_LEN, H_DIFF, 2 * DIFF_DH)),
        'cache_diff_v': nrm((N_C, DEC_BATCH, PAST_LEN, H_DIFF, 2 * DIFF_DH)),
        'g_mix': gain((DEPTH, D_MODEL)),
        'a_w_in': nrm((N_A, D_MODEL, A_IN), D_MODEL ** -0.5),
        'a_g_q': gain((N_A, Q_LORA)),
        'a_w_uq': nrm((N_A, Q_LORA, H_MLA * (MLA_NOPE + MLA_ROPE)), Q_LORA ** -0.5),
        'a_g_kv': gain((N_A, KV_LORA)),
        'a_w_ukv': nrm((N_A, KV_LORA, H_MLA * (MLA_NOPE + MLA_V)), KV_LORA ** -0.5),
        'a_b_f': forget_bias[None, :] + nrm((N_A, H_FOX), 0.1),
        'a_w_out': nrm((N_A, A_OUT, D_MODEL), A_OUT ** -0.5),
        'c_w_in': nrm((N_C, D_MODEL, 3 * C_W), D_MODEL ** -0.5),
        'c_lam': nrm((N_C, 4, DIFF_DH), 0.1),
        'c_g_sub': gain((N_C, 2 * DIFF_DH)),
        'c_w_out': nrm((N_C, C_W, D_MODEL), C_W ** -0.5),
        'g_mlp': gain((DEPTH, D_MODEL)),
        'w_up': nrm((DEPTH, D_MODEL, D_FF), D_MODEL ** -0.5),
        'w_down': nrm((DEPTH, D_FF, D_MODEL), D_FF ** -0.5),
        'g_final': gain((D_MODEL,)),
    }


def reference(x_prompt, x_sample, cache_mla_ckv, cache_mla_krope, cache_fox_k, cache_fox_v,
              cache_fox_logf, cache_diff_k, cache_diff_v, g_mix, a_w_in, a_g_q, a_w_uq, a_g_kv,
              a_w_ukv, a_b_f, a_w_out, c_w_in, c_lam, c_g_sub, c_w_out, g_mlp, w_up, w_down,
              g_final):
    past_len = cache_mla_ckv.shape[2]
    pos_p = jnp.arange(x_prompt.shape[1])
    pos_s = past_len + jnp.arange(x_sample.shape[1])
    xp, xs = x_prompt, x_sample
    a_p, a_s, c_p, c_s = [], [], [], []
    for layer in range(DEPTH):
        i = layer // 2
        hp = _rms(xp, g_mix[layer])
        hs = _rms(xs, g_mix[layer])
        if layer % 2 == 0:
            wa = (a_w_in[i], a_g_q[i], a_w_uq[i], a_g_kv[i], a_w_ukv[i], a_b_f[i], a_w_out[i])
            past = (cache_mla_ckv[i], cache_mla_krope[i], cache_fox_k[i], cache_fox_v[i], cache_fox_logf[i])
            mp, rp = _mixer_a(hp, pos_p, None, *wa)
            ms, rs = _mixer_a(hs, pos_s, past, *wa)
            a_p.append(rp)
            a_s.append(rs)
        else:
            lam_init = 0.8 - 0.6 * math.exp(-0.3 * layer)
            wc = (c_w_in[i], c_lam[i], c_g_sub[i], c_w_out[i], lam_init)
            past = (cache_diff_k[i], cache_diff_v[i])
            mp, rp = _mixer_c(hp, pos_p, None, *wc)
            ms, rs = _mixer_c(hs, pos_s, past, *wc)
            c_p.append(rp)
            c_s.append(rs)
        xp = xp + mp
        xs = xs + ms
        xp = xp + _mlp(_rms(xp, g_mlp[layer]), w_up[layer], w_down[layer])
        xs = xs + _mlp(_rms(xs, g_mlp[layer]), w_up[layer], w_down[layer])
    y_prompt = _rms(xp, g_final)
    y_sample = _rms(xs, g_final)
    return (y_prompt, y_sample,
            _stack(a_p, 0), _stack(a_p, 1), _stack(a_p, 2), _stack(a_p, 3), _stack(a_p, 4),
            _stack(c_p, 0), _stack(c_p, 1),
            _stack(a_s, 0), _stack(a_s, 1), _stack(a_s, 2), _stack(a_s, 3), _stack(a_s, 4),
            _stack(c_s, 0), _stack(c_s, 1))
```

```python
import math
from contextlib import ExitStack
import numpy as np
import concourse.bass as bass
import concourse.mybir as mybir
from concourse.bass_utils import run_bass_kernel_spmd

F32 = mybir.dt.float32
BF16 = mybir.dt.bfloat16
I32 = mybir.dt.int32
AF = mybir.ActivationFunctionType
ALU = mybir.AluOpType

NCORES = 8
D = 1024
SEQ = 8192
DEC_SEQ = 32
PAST = 4096
NS = 2
CHUNK = 64
THETA = 500000.0
Q_LORA, KV_LORA, MLA_ROPE, MLA_NOPE, MLA_V = 512, 256, 32, 64, 64
H = 8
A_IN = 2344
D_FF = 4096
LS = PAST + DEC_SEQ
LSP = 33 * 128
STAGES = 6
DBG_TILES = None
DBG_CACHE = None
DBG_STEP = 99
KV_STEP = 99
KV_SUB = 99
DBG_GROUPS = None
DBG_CORES = NCORES if False else 8

ENGINES = ("tensor", "vector", "scalar", "gpsimd", "sync")


class Buf:
    __slots__ = ("name", "w", "r")

    def __init__(self, name=""):
        self.name = name
        self.w = None
        self.r = []


class Sched:
    def __init__(self, nc, stack, n_dma_sems=56):
        self.nc = nc
        self.n_dma_sems = n_dma_sems
        self.esem = {e: stack.enter_context(nc.semaphore(f"es_{e}")) for e in ENGINES}
        self.dsem = [stack.enter_context(nc.semaphore(f"ds_{i}")) for i in range(n_dma_sems)]
        self.base = {e: 0 for e in ENGINES}
        self.dma_cnt = [0] * n_dma_sems
        self._reset()

    def _reset(self):
        self.ops = {e: [] for e in ENGINES}
        self.known = {e: {} for e in ENGINES}
        self.flag = {e: set() for e in ENGINES}

    def _need(self, eng, ev, waits):
        if ev is None:
            return
        kind, key, idx = ev
        if kind == "e" and key == eng and eng == "tensor":
            return
        k = (kind, key)
        if self.known[eng].get(k, -1) >= idx:
            return
        self.known[eng][k] = idx
        waits.append(ev)
        if kind == "e":
            self.flag[key].add(idx)

    def _deps(self, eng, reads, writes):
        waits = []
        for b in reads:
            self._need(eng, b.w, waits)
        for b in writes:
            self._need(eng, b.w, waits)
            for ev in b.r:
                self._need(eng, ev, waits)
        return waits

    def _mark(self, ev, reads, writes):
        for b in writes:
            b.w = ev
            b.r = []
        for b in reads:
            if b not in writes:
                b.r.append(ev)

    def op(self, eng, fn, reads=(), writes=()):
        waits = self._deps(eng, reads, writes)
        idx = len(self.ops[eng])
        self.ops[eng].append(dict(waits=waits, fn=fn, kind="op"))
        ev = ("e", eng, idx)
        self._mark(ev, reads, writes)
        return ev

    def dma(self, eng, fn, reads=(), writes=(), dsem=0):
        waits = self._deps(eng, reads, writes)
        self.dma_cnt[dsem] += 1
        self.ops[eng].append(dict(waits=waits, fn=fn, kind="dma", dsem=dsem))
        ev = ("d", dsem, self.dma_cnt[dsem])
        self._mark(ev, reads, writes)
        return ev

    def barrier(self):
        last = {}
        for e in ENGINES:
            n = len(self.ops[e])
            idx = None
            for i in range(n - 1, -1, -1):
                if self.ops[e][i]["kind"] == "op":
                    idx = i
                    break
            last[e] = idx
        for e in ENGINES:
            waits = []
            for f in ENGINES:
                if last[f] is not None:
                    self._need(e, ("e", f, last[f]), waits) if not (f == e == "tensor") else None
            for d in range(self.n_dma_sems):
                if self.dma_cnt[d]:
                    self._need(e, ("d", d, self.dma_cnt[d]), waits)
            self.ops[e].append(dict(waits=waits, fn=None, kind="wait"))

    def emit(self):
        nc = self.nc
        val = {}
        for e in ENGINES:
            fl = sorted(self.flag[e])
            val[e] = {idx: self.base[e] + i + 1 for i, idx in enumerate(fl)}
        sched = self
        with nc.Block() as block:
            def run(engname, engobj):
                for idx, o in enumerate(sched.ops[engname]):
                    for (kind, key, n) in o["waits"]:
                        if kind == "e":
                            engobj.wait_ge(sched.esem[key], val[key][n])
                        else:
                            engobj.wait_ge(sched.dsem[key], 16 * n)
                    if o["kind"] == "op":
                        ins = o["fn"](engobj)
                        if idx in sched.flag[engname]:
                            ins.then_inc(sched.esem[engname], 1)
                    elif o["kind"] == "dma":
                        o["fn"](engobj).then_inc(sched.dsem[o["dsem"]], 16)

            @block.tensor
            def _(e):
                run("tensor", e)

            @block.vector
            def _(e):
                run("vector", e)

            @block.scalar
            def _(e):
                run("scalar", e)

            @block.gpsimd
            def _(e):
                run("gpsimd", e)

            @block.sync
            def _(e):
                run("sync", e)
        for e in ENGINES:
            self.base[e] += len(self.flag[e])
        self._reset()


class Ctx:
    pass


def sb(ph, nc, name, shape, dt):
    return ph.enter_context(nc.sbuf_tensor(name, shape, dt))


def ps(ph, nc, name, shape, dt):
    return ph.enter_context(nc.psum_tensor(name, shape, dt))


def bc_rows(handle, offset, ncols, nparts=128):
    return bass.AP(tensor=handle, offset=offset, ap=[[0, nparts], [1, ncols]])


class Rot:
    def __init__(self, tiles, name):
        self.t = tiles
        self.b = [Buf(f"{name}{i}") for i in range(len(tiles))]
        self.i = -1

    def next(self):
        self.i = (self.i + 1) % len(self.t)
        return self.t[self.i], self.b[self.i]


def build_program():
    nc = bass.Bass("TRN2", target_bir_lowering=False)
    dt_in = {}

    def din(name, shape):
        h = nc.dram_tensor(name, list(shape), F32, kind="ExternalInput")
        dt_in[name] = h
        return h

    def dout(name, shape):
        return nc.dram_tensor(name, list(shape), F32, kind="ExternalOutput")

    def dscr(name, shape, dt):
        return nc.dram_tensor(name, list(shape), dt, kind="Internal")

    X = Ctx()
    X.xp = din("xp", [SEQ, D])
    X.xs = din("xs", [NS * DEC_SEQ, D])
    X.c_ckv = din("c_ckv", [NS, PAST, KV_LORA])
    X.c_krope = din("c_krope", [NS, PAST, MLA_ROPE])
    X.c_fk = din("c_fk", [NS, PAST, 512])
    X.c_fv = din("c_fv", [NS, PAST, 512])
    X.c_logf = din("c_logf", [NS, PAST, 8])
    X.c_dk = din("c_dk", [NS, PAST, 1024])
    X.c_dv = din("c_dv", [NS, PAST, 1024])
    X.g_mix = din("g_mix", [2, D])
    X.a_w_in = din("a_w_in", [D, A_IN])
    X.a_g_q = din("a_g_q", [1, Q_LORA])
    X.a_w_uq = din("a_w_uq", [Q_LORA, 768])
    X.a_g_kv = din("a_g_kv", [1, KV_LORA])
    X.a_w_ukv = din("a_w_ukv", [KV_LORA, 1024])
    X.a_b_f = din("a_b_f", [1, 8])
    X.a_w_out = din("a_w_out", [D, D])
    X.c_w_in = din("c_w_in", [D, 3072])
    X.c_lam = din("c_lam", [1, 256])
    X.c_g_sub = din("c_g_sub", [128, 1])
    X.c_w_out = din("c_w_out", [D, D])
    X.g_mlp = din("g_mlp", [2, D])
    X.w_up = din("w_up", [2, D, D_FF])
    X.w_down = din("w_down", [2, D_FF, D])
    X.g_final = din("g_final", [1, D])

    O = Ctx()
    O.y = [dout("y_p", [SEQ, D]), dout("y_s", [NS * DEC_SEQ, D])]
    O.ckv = [dout("o_ckv_p", [SEQ, 256]), dout("o_ckv_s", [64, 256])]
    O.krope = [dout("o_krope_p", [SEQ, 32]), dout("o_krope_s", [64, 32])]
    O.fk = [dout("o_fk_p", [SEQ, 512]), dout("o_fk_s", [64, 512])]
    O.fv = [dout("o_fv_p", [SEQ, 512]), dout("o_fv_s", [64, 512])]
    O.logf = [dout("o_logf_p", [SEQ, 8]), dout("o_logf_s", [64, 8])]
    O.dk = [dout("o_dk_p", [SEQ, 1024]), dout("o_dk_s", [64, 1024])]
    O.dv = [dout("o_dv_p", [SEQ, 1024]), dout("o_dv_s", [64, 1024])]

    O.dbg = [dout(f"dbg{i}", [128, 1024]) for i in range(4)] if DBG_STEP == 3 else []
    PL = SEQ if DBG_TILES is None else DBG_TILES * 128
    SEQS = [dict(Tq=PL, L=PL, LP=SEQ, past=0), dict(Tq=DEC_SEQ, L=LS, LP=LSP, past=PAST),
            dict(Tq=DEC_SEQ, L=LS, LP=LSP, past=PAST)]
    SC = []
    for s, q in enumerate(SEQS):
        c = Ctx()
        TqP = max(q["Tq"], 32)
        c.qmT = dscr(f"qmT{s}", [8, 96, TqP], BF16)
        c.kmT = dscr(f"kmT{s}", [8, 96, q["LP"]], BF16)
        c.v0 = dscr(f"v0_{s}", [q["LP"], 1024], BF16)
        c.qfT = dscr(f"qfT{s}", [4, 128, TqP], BF16)
        c.kfT = dscr(f"kfT{s}", [4, 128, q["LP"]], BF16)
        c.cum = dscr(f"cum{s}", [q["LP"], 8], F32)
        c.qdT = dscr(f"qdT{s}", [8, 128, TqP], BF16)
        c.kdT = dscr(f"kdT{s}", [8, 128, q["LP"]], BF16)
        c.v1 = dscr(f"v1_{s}", [q["LP"], 1024], BF16)
        c.oT = dscr(f"oT{s}", [1024, TqP], BF16)
        SC.append(c)
    x1p = dscr("x1p", [SEQ, D], F32)
    x1s = dscr("x1s", [NS * DEC_SEQ, D], F32)

    top = ExitStack()
    with top:
        S = Sched(nc, top)
        ident_f = sb(top, nc, "ident_f", [128, 128], F32)
        ident = sb(top, nc, "ident", [128, 128], BF16)
        tri = sb(top, nc, "tri", [128, 128], F32)
        tri_s = sb(top, nc, "tri_s", [64, 64], F32)
        ones_f = sb(top, nc, "ones_f", [128, 128], F32)
        ones_b = sb(top, nc, "ones_b", [128, 128], BF16)
        tri_b = sb(top, nc, "tri_b", [128, 128], BF16)
        mean_b = sb(top, nc, "mean_b", [128, 128], BF16)
        cosT = sb(top, nc, "cosT", [128, 64, 16], F32)
        sinT = sb(top, nc, "sinT", [128, 64, 16], F32)
        cosS = sb(top, nc, "cosS", [64, 16], F32)
        sinS = sb(top, nc, "sinS", [64, 16], F32)
        negpi = sb(top, nc, "negpi", [128, 1], F32)
        onec = sb(top, nc, "onec", [128, 1], F32)
        eps6 = sb(top, nc, "eps6", [128, 1], F32)
        eps5 = sb(top, nc, "eps5", [128, 1], F32)
        B_const = Buf("const")

        with ExitStack() as ph:
            posi = sb(ph, nc, "posi", [128, 64], I32)
            posf = sb(ph, nc, "posf", [128, 64], F32)
            ang = sb(ph, nc, "ang", [128, 64, 16], F32)
            ang2 = sb(ph, nc, "ang2", [128, 64, 16], F32)
            possi = sb(ph, nc, "possi", [64, 1], I32)
            possf = sb(ph, nc, "possf", [64, 1], F32)
            angs = sb(ph, nc, "angs", [64, 16], F32)
            angs2 = sb(ph, nc, "angs2", [64, 16], F32)
            Bp = Buf("p0")
            G = "gpsimd"
            S.op(G, lambda e: e.memset(ones_f[:], 1.0), writes=[Bp])
            S.op(G, lambda e: e.memset(ones_b[:], 1.0), writes=[Bp])
            S.op(G, lambda e: e.memset(mean_b[:], 1.0 / 128.0), writes=[Bp])
            S.op(G, lambda e: e.memset(negpi[:], -math.pi), writes=[Bp])
            S.op(G, lambda e: e.memset(onec[:], 1.0), writes=[Bp])
            S.op(G, lambda e: e.memset(eps6[:], 1e-6), writes=[Bp])
            S.op(G, lambda e: e.memset(eps5[:], 1e-5), writes=[Bp])
            S.op(G, lambda e: e.affine_select(out=ident_f[:], in_=ones_f[:], pattern=[[1, 128]],
                                              compare_op=ALU.is_equal, fill=0.0, base=0, channel_multiplier=-1),
                 reads=[Bp], writes=[Bp])
            S.op(G, lambda e: e.tensor_copy(out=ident[:], in_=ident_f[:]), reads=[Bp], writes=[Bp])
            S.op(G, lambda e: e.affine_select(out=tri[:], in_=ones_f[:], pattern=[[1, 128]],
                                              compare_op=ALU.is_ge, fill=0.0, base=0, channel_multiplier=-1),
                 reads=[Bp], writes=[Bp])
            S.op(G, lambda e: e.tensor_copy(out=tri_s[:], in_=tri[0:64, 0:64]), reads=[Bp], writes=[Bp])
            S.op(G, lambda e: e.tensor_copy(out=tri_b[:], in_=tri[:]), reads=[Bp], writes=[Bp])
            S.op(G, lambda e: e.memset(tri_s[0:32, 32:64], 0.0), reads=[Bp], writes=[Bp])
            S.op(G, lambda e: e.iota(posi[:], [[128, 64]], base=0, channel_multiplier=1), writes=[Bp])
            S.op(G, lambda e: e.tensor_copy(out=posf[:], in_=posi[:]), reads=[Bp], writes=[Bp])
            S.op(G, lambda e: e.iota(possi[0:32, :], [[0, 1]], base=PAST, channel_multiplier=1), writes=[Bp])
            S.op(G, lambda e: e.iota(possi[32:64, :], [[0, 1]], base=PAST, channel_multiplier=1),
                 reads=[Bp], writes=[Bp])
            S.op(G, lambda e: e.tensor_copy(out=possf[:], in_=possi[:]), reads=[Bp], writes=[Bp])
            V = "vector"
            for f in range(16):
                inv = float(np.float32(THETA) ** np.float32(-f / 16.0))
                S.op(V, lambda e, f=f, inv=inv: e.tensor_scalar(out=ang[:, :, f], in0=posf[:], scalar1=inv,
                                                                scalar2=None, op0=ALU.mult),
                     reads=[Bp], writes=[Bp])
                S.op(V, lambda e, f=f, inv=inv: e.tensor_scalar(out=angs[:, f:f + 1], in0=possf[:], scalar1=inv,
                                                                scalar2=None, op0=ALU.mult),
                     reads=[Bp], writes=[Bp])
            twopi = 2.0 * math.pi
            ki = sb(ph, nc, "ki", [128, 64, 16], I32)
            kf = sb(ph, nc, "kf", [128, 64, 16], F32)
            kis = sb(ph, nc, "kis", [64, 16], I32)
            kfs = sb(ph, nc, "kfs", [64, 16], F32)
            PI_LO = 3.1415925

            def trig(dst, a, a2, k_i, k_f, shift):
                S.op(V, lambda e: e.tensor_scalar(out=a2, in0=a, scalar1=1.0 / twopi, scalar2=shift,
                                                  op0=ALU.mult, op1=ALU.add), reads=[Bp], writes=[Bp])
                S.op(V, lambda e: e.tensor_copy(out=k_i, in_=a2), reads=[Bp], writes=[Bp])
                S.op(V, lambda e: e.tensor_copy(out=k_f, in_=k_i), reads=[Bp], writes=[Bp])
                S.op(V, lambda e: e.tensor_tensor(out=a2, in0=a2, in1=k_f, op=ALU.subtract), reads=[Bp], writes=[Bp])
                S.op(V, lambda e: e.tensor_scalar(out=a2, in0=a2, scalar1=twopi, scalar2=PI_LO,
                                                  op0=ALU.mult, op1=ALU.min), reads=[Bp], writes=[Bp])
                S.op(V, lambda e: e.tensor_scalar(out=a2, in0=a2, scalar1=-PI_LO, scalar2=None,
                                                  op0=ALU.max), reads=[Bp], writes=[Bp])
                S.op("scalar", lambda e: e.activation(out=dst, in_=a2, func=AF.Sin), reads=[Bp], writes=[Bp])

            trig(sinT[:], ang[:], ang2[:], ki[:], kf[:], 0.0)
            trig(cosT[:], ang[:], ang2[:], ki[:], kf[:], 0.25)
            trig(sinS[:], angs[:], angs2[:], kis[:], kfs[:], 0.0)
            trig(cosS[:], angs[:], angs2[:], kis[:], kfs[:], 0.25)
            S.barrier()
            S.emit()

        K = Ctx()
        K.nc, K.S, K.X, K.O, K.SC, K.SEQS = nc, S, X, O, SC, SEQS
        K.x1 = [x1p, x1s]
        K.ident, K.ident_f, K.tri, K.ones_f, K.ones_b, K.mean_b = ident, ident_f, tri, ones_f, ones_b, mean_b
        K.cosT, K.sinT, K.cosS, K.sinS, K.onec = cosT, sinT, cosS, sinS, onec
        K.eps6, K.eps5 = eps6, eps5
        K.tri_b = tri_b
        stages = STAGES
        if stages >= 1:
            phase_proj0(K)
        if stages >= 2:
            phase_attn(K, 0)
        if stages >= 3:
            phase_mlp(K, 0)
        if stages >= 4:
            phase_proj1(K)
        if stages >= 5:
            phase_attn(K, 1)
        if stages >= 6:
            phase_mlp(K, 1)
    return nc


class T:
    _next_ds = [0]

    _uid = [0]

    def __init__(self, K, ph, name, shape, dt, psum=False):
        T._uid[0] += 1
        name = f"{name}_{T._uid[0]}"
        self.t = (ps if psum else sb)(ph, K.nc, name, shape, dt)
        self.b = Buf(name)
        self._ds = None

    @property
    def ds(self):
        if self._ds is None:
            self._ds = T._next_ds[0] % 56
            T._next_ds[0] += 1
        return self._ds


def dap(handle, off, rowstride, nrows, ncols):
    return bass.AP(tensor=handle, offset=off, ap=[[rowstride, nrows], [1, ncols]])


def bcast_mid(a, n):
    return bass.AP(tensor=a.tensor, offset=a.offset, ap=[list(a.ap[0]), [0, n], list(a.ap[1])])


def load_w(K, handle, off, rowstride, nk, ncols, dst, stg, Bw):
    S = K.S
    engs = ("vector", "gpsimd")
    i = 0
    CH = stg[0].t.shape[1]
    for k in range(nk):
        for c0 in range(0, ncols, CH):
            cw = min(CH, ncols - c0)
            st = stg[i % len(stg)]
            src = dap(handle, off + k * 128 * rowstride + c0, rowstride, 128, cw)
            S.dma("sync", lambda e, st=st, src=src, cw=cw: e.dma_start(out=st.t[:, 0:cw], in_=src),
                  writes=[st.b], dsem=st.ds)
            eng = engs[i % 2]
            S.op(eng, lambda e, st=st, k=k, c0=c0, cw=cw: e.tensor_copy(out=dst.t[:, k, c0:c0 + cw],
                                                                        in_=st.t[:, 0:cw]),
                 reads=[st.b], writes=[Bw[i % 2]])
            i += 1


def load_bc(K, handle, off, ncols, dst):
    K.S.dma("sync", lambda e: e.dma_start(out=dst.t[:, 0:ncols], in_=bc_rows(handle, off, ncols)),
            writes=[dst.b], dsem=dst.ds)


def rms_stats(K, src_ap, n, W, eps, junk, ss, rs, src_bufs):
    S = K.S
    S.op("gpsimd", lambda e: e.memset(ss.t[0:n, 0:1], 0.0), writes=[ss.b])
    S.op("scalar", lambda e: e.activation(out=junk.t[0:n, 0:W], in_=src_ap, func=AF.Square,
                                          accum_out=ss.t[0:n, 0:1]),
         reads=src_bufs + [ss.b], writes=[junk.b, ss.b])
    epsc = K.eps6 if eps < 5e-6 else K.eps5
    S.op("scalar", lambda e: e.activation(out=rs.t[0:n, 0:1], in_=ss.t[0:n, 0:1], func=AF.Ln, scale=1.0 / W,
                                          bias=epsc[0:n, :]), reads=[ss.b], writes=[rs.b])
    S.op("scalar", lambda e: e.activation(out=rs.t[0:n, 0:1], in_=rs.t[0:n, 0:1], func=AF.Exp, scale=-0.5),
         reads=[rs.b], writes=[rs.b])


def transpose_blocks(K, src, n, blocks, ptile, dst, evac_eng):
    S = K.S
    pv = ptile.t[:].rearrange("p (b c) -> p b c", c=128)
    src2 = src.t[:] if len(src.t.shape) == 2 else src.t[:].rearrange("p h c -> p (h c)")
    wmax = max(w for _, w in blocks)
    for b, (c0, w) in enumerate(blocks):
        S.op("tensor", lambda e, b=b, c0=c0, w=w: e.transpose(out=pv[0:w, b, 0:n], in_=src2[0:n, c0:c0 + w],
                                                             identity=K.ident[0:n, 0:n]),
             reads=[src.b], writes=[ptile.b])
    nb = len(blocks)
    if evac_eng == "scalar":
        S.op("scalar", lambda e: e.activation(out=dst.t[0:wmax, 0:nb, 0:n], in_=pv[0:wmax, 0:nb, 0:n], func=AF.Copy),
             reads=[ptile.b], writes=[dst.b])
    else:
        S.op("vector", lambda e: e.tensor_copy(out=dst.t[0:wmax, 0:nb, 0:n], in_=pv[0:wmax, 0:nb, 0:n]),
             reads=[ptile.b], writes=[dst.b])


def rope(K, x1, x2, o1, o2, cos, sin, t1, t2, rbufs, wbufs):
    S = K.S
    tb = [K.ropeb]
    V = "vector"
    S.op(V, lambda e: e.tensor_tensor(out=t1, in0=x1, in1=cos, op=ALU.mult), reads=rbufs, writes=tb)
    S.op(V, lambda e: e.tensor_tensor(out=t2, in0=x2, in1=sin, op=ALU.mult), reads=rbufs, writes=tb)
    S.op(V, lambda e: e.tensor_tensor(out=o1, in0=t1, in1=t2, op=ALU.subtract), reads=tb, writes=wbufs)
    S.op(V, lambda e: e.tensor_tensor(out=t1, in0=x2, in1=cos, op=ALU.mult), reads=rbufs + wbufs, writes=tb)
    S.op(V, lambda e: e.tensor_tensor(out=t2, in0=x1, in1=sin, op=ALU.mult), reads=rbufs, writes=tb)
    S.op(V, lambda e: e.tensor_tensor(out=o2, in0=t1, in1=t2, op=ALU.add), reads=tb, writes=wbufs)


def token_tiles(K):
    tiles = []
    for i in range(SEQ // 128 if DBG_TILES is None else DBG_TILES):
        tiles.append(dict(n=128, seq=0, pos0=i * 128, xi=0, row0=i * 128, oi=0, ti=i))
    for s in range(NS):
        tiles.append(dict(n=DEC_SEQ, seq=1 + s, pos0=PAST, xi=1, row0=s * DEC_SEQ, oi=1, ti=None))
    return tiles


def rope_tables(K, tl, n):
    if tl["ti"] is not None:
        return K.cosT[0:n, tl["ti"], :], K.sinT[0:n, tl["ti"], :]
    return K.cosS[0:n, :], K.sinS[0:n, :]


def store(K, dst_ap, src_ap, src_t):
    K.S.dma("sync", lambda e: e.dma_start(out=dst_ap, in_=src_ap), reads=[src_t.b], dsem=src_t.ds)


def phase_proj0(K):
    nc, S, X, O, SC = K.nc, K.S, K.X, K.O, K.SC
    with ExitStack() as ph:
        mk = lambda name, shape, dt, psum=False: T(K, ph, name, shape, dt, psum)
        K.ropeb = Buf("ropetmp")
        w_in = mk("w_in", [128, 8, A_IN], BF16)
        w_uq = mk("w_uq", [128, 4, 768], BF16)
        w_ukv = mk("w_ukv", [128, 2, 1024], BF16)
        gmix, gq, gkv, bfb = mk("gmix", [128, D], F32), mk("gq", [128, 512], F32), mk("gkv", [128, 256], F32), \
            mk("bfb", [128, 8], F32)
        stg = [mk(f"wst{i}", [128, 1024], F32) for i in range(2)]
        Bw = [Buf("w0"), Buf("w1")]
        load_w(K, X.a_w_in, 0, A_IN, 8, A_IN, w_in, stg, Bw)
        load_w(K, X.a_w_uq, 0, 768, 4, 768, w_uq, stg, Bw)
        load_w(K, X.a_w_ukv, 0, 1024, 2, 1024, w_ukv, stg, Bw)
        load_bc(K, X.g_mix, 0, D, gmix)
        load_bc(K, X.a_g_q, 0, 512, gq)
        load_bc(K, X.a_g_kv, 0, 256, gkv)
        load_bc(K, X.a_b_f, 0, 8, bfb)
        WB = Bw + [gmix.b, gq.b, gkv.b, bfb.b]

        xt = [mk(f"xt{i}", [128, D], F32) for i in range(2)]
        junk = mk("junk", [128, D], BF16)
        ss, rs = mk("ss", [128, 1], F32), mk("rs", [128, 1], F32)
        xn = mk("xn", [128, D], BF16)
        hT = mk("hT", [128, 8, 128], BF16)
        qn = mk("qn", [128, 512], BF16)
        qnT = mk("qnT", [128, 4, 128], BF16)
        ckv_f, ckv_b = mk("ckv_f", [128, 256], F32), mk("ckv_b", [128, 256], BF16)
        ckvT = mk("ckvT", [128, 2, 128], BF16)
        kr_f, kr_b = mk("kr_f", [128, 32], F32), mk("kr_b", [128, 32], BF16)
        t1, t2 = mk("t1", [128, 8, 16], F32), mk("t2", [128, 8, 16], F32)
        q_f = mk("q_f", [128, 8, 96], F32)
        qm_b, km_b = mk("qm_b", [128, 8, 96], BF16), mk("km_b", [128, 8, 96], BF16)
        vcat = mk("vcat", [128, 1024], BF16)
        fq_b, fk_b = mk("fq_b", [128, 512], BF16), mk("fk_b", [128, 512], BF16)
        fk_f, fv_f = mk("fk_f", [128, 512], F32), mk("fv_f", [128, 512], F32)
        z, logf, cum_t = mk("z", [128, 8], F32), mk("logf", [128, 8], F32), mk("cum_t", [128, 8], F32)
        carry = [mk(f"carry{s}", [128, 8], F32) for s in range(3)]
        qmT_s, kmT_s = mk("qmT_s", [96, 8, 128], BF16), mk("kmT_s", [96, 8, 128], BF16)
        qfT_s, kfT_s = mk("qfT_s", [128, 4, 128], BF16), mk("kfT_s", [128, 4, 128], BF16)
        cst = mk("cst", [128, 1024], F32)
        lsp, lres = mk("lsp", [128, 24], BF16), mk("lres", [128, 8], F32)
        pb = [mk(f"pb{i}", [128, 512], F32, True) for i in range(6)]
        pt = [mk(f"pt{i}", [128, 1024], BF16, True) for i in range(2)]
        for s in range(3):
            S.op("gpsimd", lambda e, s=s: e.memset(carry[s].t[:], 0.0), writes=[carry[s].b])

        def kvside(n, seq, kpos0, lf_t):
            sc = SC[seq]
            LP = K.SEQS[seq]["LP"]
            transpose_blocks(K, ckv_b, n, [(0, 128), (128, 128)], pt[1], ckvT, "scalar")
            if KV_STEP < 1:
                return
            for half in range(2):
                for c in range(2):
                    S.op("tensor", lambda e, half=half, c=c: e.matmul(
                        pb[1 + half].t[0:n, :], lhsT=ckvT.t[:, c, 0:n], rhs=w_ukv.t[:, c, half * 512:(half + 1) * 512],
                        start=(c == 0), stop=(c == 1)), reads=[ckvT.b] + WB, writes=[pb[1 + half].b])
            for half in range(2):
                if KV_SUB < 1:
                    break
                pv = pb[1 + half].t[:].rearrange("p (h c) -> p h c", c=128)
                S.op("vector", lambda e, half=half, pv=pv: e.tensor_copy(
                    out=km_b.t[0:n, half * 4:(half + 1) * 4, 0:64], in_=pv[0:n, :, 0:64]),
                    reads=[pb[1 + half].b], writes=[km_b.b])
                if KV_SUB < 2:
                    continue
                vv = vcat.t[:, 0:512].rearrange("p (h c) -> p h c", c=64)
                S.op("vector", lambda e, half=half, pv=pv, vv=vv: e.tensor_copy(
                    out=vv[0:n, half * 4:(half + 1) * 4, :], in_=pv[0:n, :, 64:128]),
                    reads=[pb[1 + half].b], writes=[vcat.b])
            if KV_STEP < 2:
                return
            for hh in range(8):
                S.op("gpsimd", lambda e, hh=hh: e.tensor_copy(out=km_b.t[0:n, hh, 64:96], in_=kr_b.t[0:n, :]),
                     reads=[kr_b.b], writes=[km_b.b])
            if KV_STEP < 3:
                return
            transpose_blocks(K, km_b, n, [(h * 96, 96) for h in range(8)], pt[0], kmT_s, "vector")
            if KV_STEP < 4:
                return
            store(K, bass.AP(tensor=sc.kmT, offset=kpos0, ap=[[LP, 96], [96 * LP, 8], [1, n]]),
                  kmT_s.t[0:96, :, 0:n], kmT_s)
            if KV_STEP < 5:
                return
            transpose_blocks(K, fk_b, n, [(b * 128, 128) for b in range(4)], pt[1], kfT_s, "scalar")
            store(K, bass.AP(tensor=sc.kfT, offset=kpos0, ap=[[LP, 128], [128 * LP, 4], [1, n]]),
                  kfT_s.t[:, :, 0:n], kfT_s)
            if KV_STEP < 6:
                return
            store(K, dap(sc.v0, kpos0 * 1024, 1024, n, 1024), vcat.t[0:n, :], vcat)
            if KV_STEP < 7:
                return
            cr = carry[seq]
            S.op("vector", lambda e: e.tensor_copy(out=lsp.t[0:n, 0:8], in_=lf_t.t[0:n, 0:8]), reads=[lf_t.b],
                 writes=[lsp.b])
            S.op("vector", lambda e: e.tensor_tensor(out=lres.t[0:n, :], in0=lf_t.t[0:n, 0:8], in1=lsp.t[0:n, 0:8],
                                                     op=ALU.subtract), reads=[lf_t.b, lsp.b], writes=[lres.b])
            S.op("vector", lambda e: e.tensor_copy(out=lsp.t[0:n, 8:16], in_=lres.t[0:n, :]), reads=[lres.b],
                 writes=[lsp.b])
            S.op("vector", lambda e: e.tensor_tensor(out=lres.t[0:n, :], in0=lres.t[0:n, :], in1=lsp.t[0:n, 8:16],
                                                     op=ALU.subtract), reads=[lres.b, lsp.b], writes=[lres.b])
            S.op("vector", lambda e: e.tensor_copy(out=lsp.t[0:n, 16:24], in_=lres.t[0:n, :]), reads=[lres.b],
                 writes=[lsp.b])
            for k3 in range(3):
                S.op("tensor", lambda e, k3=k3: e.matmul(pb[3].t[0:n, 0:8], lhsT=K.tri_b[0:n, 0:n],
                                                         rhs=lsp.t[0:n, k3 * 8:(k3 + 1) * 8],
                                                         start=(k3 == 0), stop=(k3 == 2)),
                     reads=[lsp.b], writes=[pb[3].b])
            for k3 in range(3):
                S.op("tensor", lambda e, k3=k3: e.matmul(pb[3].t[:, 8:16], lhsT=K.ones_b[0:n, :],
                                                         rhs=lsp.t[0:n, k3 * 8:(k3 + 1) * 8],
                                                         start=(k3 == 0), stop=(k3 == 2)),
                     reads=[lsp.b], writes=[pb[3].b])
            S.op("vector", lambda e: e.tensor_tensor(out=cum_t.t[0:n, :], in0=pb[3].t[0:n, 0:8], in1=cr.t[0:n, :],
                                                     op=ALU.add), reads=[pb[3].b, cr.b], writes=[cum_t.b])
            S.op("vector", lambda e: e.tensor_tensor(out=cr.t[:], in0=pb[3].t[:, 8:16], in1=cr.t[:], op=ALU.add),
                 reads=[pb[3].b, cr.b], writes=[cr.b])
            if KV_STEP < 8:
                return
            store(K, dap(sc.cum, kpos0 * 8, 8, n, 8), cum_t.t[0:n, :], cum_t)

        for s in range(NS):
            for i in range(PAST // 128 if DBG_CACHE is None else DBG_CACHE):
                r0 = (s * PAST + i * 128)
                S.dma("sync", lambda e, r0=r0: e.dma_start(out=cst.t[:, 0:256], in_=dap(X.c_ckv, r0 * 256, 256, 128, 256)),
                      writes=[cst.b], dsem=cst.ds)
                S.op("vector", lambda e: e.tensor_copy(out=ckv_b.t[:], in_=cst.t[:, 0:256]), reads=[cst.b],
                     writes=[ckv_b.b])
                S.dma("sync", lambda e, r0=r0: e.dma_start(out=kr_f.t[:], in_=dap(X.c_krope, r0 * 32, 32, 128, 32)),
                      writes=[kr_f.b], dsem=kr_f.ds)
                S.op("vector", lambda e: e.tensor_copy(out=kr_b.t[:], in_=kr_f.t[:]), reads=[kr_f.b], writes=[kr_b.b])
                S.dma("sync", lambda e, r0=r0: e.dma_start(out=fk_f.t[:], in_=dap(X.c_fk, r0 * 512, 512, 128, 512)),
                      writes=[fk_f.b], dsem=fk_f.ds)
                S.op("gpsimd", lambda e: e.tensor_copy(out=fk_b.t[:], in_=fk_f.t[:]), reads=[fk_f.b], writes=[fk_b.b])
                S.dma("sync", lambda e, r0=r0: e.dma_start(out=fv_f.t[:], in_=dap(X.c_fv, r0 * 512, 512, 128, 512)),
                      writes=[fv_f.b], dsem=fv_f.ds)
                S.op("gpsimd", lambda e: e.tensor_copy(out=vcat.t[:, 512:1024], in_=fv_f.t[:]), reads=[fv_f.b],
                     writes=[vcat.b])
                S.dma("sync", lambda e, r0=r0: e.dma_start(out=logf.t[:], in_=dap(X.c_logf, r0 * 8, 8, 128, 8)),
                      writes=[logf.b], dsem=logf.ds)
                kvside(128, 1 + s, i * 128, logf)

        tiles = token_tiles(K)
        xh = [X.xp, X.xs]
        for it, tl in enumerate(tiles):
            n, seq, pos0, row0, oi = tl["n"], tl["seq"], tl["pos0"], tl["row0"], tl["oi"]
            xb = xt[it % 2]
            S.dma("sync", lambda e, xb=xb, tl=tl, n=n: e.dma_start(out=xb.t[0:n, :],
                                                                   in_=dap(xh[tl["xi"]], tl["row0"] * D, D, n, D)),
                  writes=[xb.b], dsem=xb.ds)
            if DBG_STEP < 1:
                continue
            rms_stats(K, xb.t[0:n, :], n, D, 1e-6, junk, ss, rs, [xb.b])
            S.op("vector", lambda e, xb=xb, n=n: e.scalar_tensor_tensor(out=xn.t[0:n, :], in0=xb.t[0:n, :],
                                                                        scalar=rs.t[0:n, 0:1], in1=gmix.t[0:n, :],
                                                                        op0=ALU.mult, op1=ALU.mult),
                 reads=[xb.b, rs.b, gmix.b], writes=[xn.b])
            if DBG_STEP < 2:
                continue
            transpose_blocks(K, xn, n, [(c * 128, 128) for c in range(8)], pt[0], hT, "scalar")
            groups = [(pb[0], 0, 0, 512), (pb[1], 0, 512, 288), (pb[1], 288, 2336, 8), (pb[2], 0, 800, 512),
                      (pb[3], 0, 1312, 512), (pb[4], 0, 1824, 512)]
            for (pt_, pc0, wc0, wd) in groups:
                for c in range(8):
                    S.op("tensor", lambda e, pt_=pt_, pc0=pc0, wc0=wc0, wd=wd, c=c, n=n: e.matmul(
                        pt_.t[0:n, pc0:pc0 + wd], lhsT=hT.t[:, c, 0:n], rhs=w_in.t[:, c, wc0:wc0 + wd],
                        start=(c == 0), stop=(c == 7)), reads=[hT.b] + WB, writes=[pt_.b])
            if DBG_STEP == 3 and it == 0:
                dg = [mk(f"dg{i}", [128, 1024], F32) for i in range(4)]
                S.op("vector", lambda e: e.tensor_copy(out=dg[0].t[:], in_=xn.t[:]), reads=[xn.b], writes=[dg[0].b])
                S.op("vector", lambda e: e.tensor_copy(out=dg[1].t[:], in_=hT.t[:].rearrange("p a b -> p (a b)")), reads=[hT.b], writes=[dg[1].b])
                S.op("vector", lambda e: e.tensor_copy(out=dg[2].t[:, 0:512], in_=pb[0].t[:]), reads=[pb[0].b], writes=[dg[2].b])
                S.op("vector", lambda e: e.tensor_copy(out=dg[2].t[:, 512:1024], in_=pb[1].t[:]), reads=[pb[1].b], writes=[dg[2].b])
                S.op("vector", lambda e: e.tensor_copy(out=dg[3].t[:, 0:128], in_=K.ident[:]), reads=[], writes=[dg[3].b])
                S.op("vector", lambda e: e.tensor_copy(out=dg[3].t[:, 128:129], in_=rs.t[:]), reads=[rs.b], writes=[dg[3].b])
                S.op("vector", lambda e: e.tensor_copy(out=dg[3].t[:, 256:272], in_=K.cosT[:, 0, :]), reads=[], writes=[dg[3].b])
                S.op("vector", lambda e: e.tensor_copy(out=dg[3].t[:, 272:288], in_=K.sinT[:, 0, :]), reads=[], writes=[dg[3].b])
                S.op("vector", lambda e: e.tensor_copy(out=dg[3].t[:, 288:304], in_=K.cosT[:, 63, :]), reads=[], writes=[dg[3].b])
                for i in range(4):
                    store(K, dap(O.dbg[i], 0, 1024, 128, 1024), dg[i].t[:], dg[i])
            if DBG_STEP < 4:
                continue
            rms_stats(K, pb[0].t[0:n, :], n, 512, 1e-6, junk, ss, rs, [pb[0].b])
            S.op("vector", lambda e, n=n: e.scalar_tensor_tensor(out=qn.t[0:n, :], in0=pb[0].t[0:n, :],
                                                                 scalar=rs.t[0:n, 0:1], in1=gq.t[0:n, :],
                                                                 op0=ALU.mult, op1=ALU.mult),
                 reads=[pb[0].b, rs.b, gq.b], writes=[qn.b])
            rms_stats(K, pb[1].t[0:n, 0:256], n, 256, 1e-6, junk, ss, rs, [pb[1].b])
            S.op("vector", lambda e, n=n: e.scalar_tensor_tensor(out=ckv_f.t[0:n, :], in0=pb[1].t[0:n, 0:256],
                                                                 scalar=rs.t[0:n, 0:1], in1=gkv.t[0:n, :],
                                                                 op0=ALU.mult, op1=ALU.mult),
                 reads=[pb[1].b, rs.b, gkv.b], writes=[ckv_f.b])
            S.op("gpsimd", lambda e, n=n: e.tensor_copy(out=ckv_b.t[0:n, :], in_=ckv_f.t[0:n, :]), reads=[ckv_f.b],
                 writes=[ckv_b.b])
            store(K, dap(O.ckv[oi], row0 * 256, 256, n, 256), ckv_f.t[0:n, :], ckv_f)
            if DBG_STEP < 4:
                continue
            cos, sin = rope_tables(K, tl, n)
            rope(K, pb[1].t[0:n, 256:272], pb[1].t[0:n, 272:288], kr_f.t[0:n, 0:16], kr_f.t[0:n, 16:32], cos, sin,
                 t1.t[0:n, 0, :], t2.t[0:n, 0, :], [pb[1].b], [kr_f.b])
            S.op("gpsimd", lambda e, n=n: e.tensor_copy(out=kr_b.t[0:n, :], in_=kr_f.t[0:n, :]), reads=[kr_f.b],
                 writes=[kr_b.b])
            store(K, dap(O.krope[oi], row0 * 32, 32, n, 32), kr_f.t[0:n, :], kr_f)
            if DBG_STEP < 5:
                continue
            S.op("vector", lambda e, n=n: e.tensor_tensor(out=z.t[0:n, :], in0=pb[1].t[0:n, 288:296], in1=bfb.t[0:n, :],
                                                          op=ALU.add), reads=[pb[1].b, bfb.b], writes=[z.b])
            S.op("scalar", lambda e, n=n: e.activation(out=z.t[0:n, :], in_=z.t[0:n, :], func=AF.Exp, scale=-1.0),
                 reads=[z.b], writes=[z.b])
            S.op("scalar", lambda e, n=n: e.activation(out=z.t[0:n, :], in_=z.t[0:n, :], func=AF.Ln,
                                                       bias=K.onec[0:n, :]), reads=[z.b], writes=[z.b])
            S.op("vector", lambda e, n=n: e.tensor_scalar(out=logf.t[0:n, :], in0=z.t[0:n, :], scalar1=-1.0,
                                                          scalar2=None, op0=ALU.mult), reads=[z.b], writes=[logf.b])
            store(K, dap(O.logf[oi], row0 * 8, 8, n, 8), logf.t[0:n, :], logf)
            if DBG_STEP < 6:
                continue
            S.op("scalar", lambda e, n=n: e.activation(out=fq_b.t[0:n, :], in_=pb[2].t[0:n, :], func=AF.Copy),
                 reads=[pb[2].b], writes=[fq_b.b])
            S.op("vector", lambda e, n=n: e.tensor_copy(out=fk_f.t[0:n, :], in_=pb[3].t[0:n, :]), reads=[pb[3].b],
                 writes=[fk_f.b])
            S.op("gpsimd", lambda e, n=n: e.tensor_copy(out=fk_b.t[0:n, :], in_=fk_f.t[0:n, :]), reads=[fk_f.b],
                 writes=[fk_b.b])
            store(K, dap(O.fk[oi], row0 * 512, 512, n, 512), fk_f.t[0:n, :], fk_f)
            S.op("scalar", lambda e, n=n: e.activation(out=fv_f.t[0:n, :], in_=pb[4].t[0:n, :], func=AF.Copy),
                 reads=[pb[4].b], writes=[fv_f.b])
            S.op("gpsimd", lambda e, n=n: e.tensor_copy(out=vcat.t[0:n, 512:1024], in_=fv_f.t[0:n, :]),
                 reads=[fv_f.b], writes=[vcat.b])
            store(K, dap(O.fv[oi], row0 * 512, 512, n, 512), fv_f.t[0:n, :], fv_f)
            if DBG_STEP < 7:
                continue
            transpose_blocks(K, qn, n, [(c * 128, 128) for c in range(4)], pt[1], qnT, "scalar")
            for (pt_, wc0, wd) in [(pb[5], 0, 512), (pb[0], 512, 256)]:
                for c in range(4):
                    S.op("tensor", lambda e, pt_=pt_, wc0=wc0, wd=wd, c=c, n=n: e.matmul(
                        pt_.t[0:n, 0:wd], lhsT=qnT.t[:, c, 0:n], rhs=w_uq.t[:, c, wc0:wc0 + wd],
                        start=(c == 0), stop=(c == 3)), reads=[qnT.b] + WB, writes=[pt_.b])
            qf2 = q_f.t[:].rearrange("p h c -> p (h c)")
            S.op("vector", lambda e, n=n: e.tensor_copy(out=qf2[0:n, 0:512], in_=pb[5].t[0:n, :]), reads=[pb[5].b],
                 writes=[q_f.b])
            S.op("scalar", lambda e, n=n: e.activation(out=qf2[0:n, 512:768], in_=pb[0].t[0:n, 0:256], func=AF.Copy),
                 reads=[pb[0].b], writes=[q_f.b])
            S.op("gpsimd", lambda e, n=n: e.tensor_copy(out=qm_b.t[0:n, :, 0:64], in_=q_f.t[0:n, :, 0:64]),
                 reads=[q_f.b], writes=[qm_b.b])
            rope(K, q_f.t[0:n, :, 64:80], q_f.t[0:n, :, 80:96], qm_b.t[0:n, :, 64:80], qm_b.t[0:n, :, 80:96],
                 bcast_mid(cos, 8), bcast_mid(sin, 8), t1.t[0:n, :, :], t2.t[0:n, :, :], [q_f.b], [qm_b.b])
            sc = SC[seq]
            TqP = max(K.SEQS[seq]["Tq"], 32)
            qcol0 = pos0 - K.SEQS[seq]["past"]
            transpose_blocks(K, qm_b, n, [(h * 96, 96) for h in range(8)], pt[0], qmT_s, "vector")
            store(K, bass.AP(tensor=sc.qmT, offset=qcol0, ap=[[TqP, 96], [96 * TqP, 8], [1, n]]),
                  qmT_s.t[0:96, :, 0:n], qmT_s)
            transpose_blocks(K, fq_b, n, [(b * 128, 128) for b in range(4)], pt[1], qfT_s, "scalar")
            store(K, bass.AP(tensor=sc.qfT, offset=qcol0, ap=[[TqP, 128], [128 * TqP, 4], [1, n]]),
                  qfT_s.t[:, :, 0:n], qfT_s)
            if DBG_STEP < 8:
                continue
            kvside(n, seq, pos0, logf)
        S.barrier()
        S.emit()


_NC_CACHE = {}


def kernel(**inp):
    f = lambda a: np.ascontiguousarray(np.asarray(a, dtype=np.float32))
    if "nc" not in _NC_CACHE:
        _NC_CACHE["nc"] = build_program()
    nc = _NC_CACHE["nc"]
    shared = {
        "g_mix": f(inp["g_mix"]), "a_w_in": f(inp["a_w_in"][0]), "a_g_q": f(inp["a_g_q"][0]).reshape(1, 512),
        "a_w_uq": f(inp["a_w_uq"][0]), "a_g_kv": f(inp["a_g_kv"][0]).reshape(1, 256),
        "a_w_ukv": f(inp["a_w_ukv"][0]), "a_b_f": f(inp["a_b_f"][0]).reshape(1, 8),
        "a_w_out": f(inp["a_w_out"][0]), "c_w_in": f(inp["c_w_in"][0]), "c_lam": f(inp["c_lam"][0]).reshape(1, 256),
        "c_g_sub": f(inp["c_g_sub"][0]).reshape(128, 1), "c_w_out": f(inp["c_w_out"][0]),
        "g_mlp": f(inp["g_mlp"]), "w_up": f(inp["w_up"]), "w_down": f(inp["w_down"]),
        "g_final": f(inp["g_final"]).reshape(1, D),
    }
    in_maps = []
    for c in range(NCORES):
        sl = slice(NS * c, NS * c + NS)
        m = dict(shared)
        m["xp"] = f(inp["x_prompt"][c])
        m["xs"] = f(inp["x_sample"][sl]).reshape(NS * DEC_SEQ, D)
        m["c_ckv"] = f(inp["cache_mla_ckv"][0, sl])
        m["c_krope"] = f(inp["cache_mla_krope"][0, sl])
        m["c_fk"] = f(inp["cache_fox_k"][0, sl]).reshape(NS, PAST, 512)
        m["c_fv"] = f(inp["cache_fox_v"][0, sl]).reshape(NS, PAST, 512)
        m["c_logf"] = f(inp["cache_fox_logf"][0, sl])
        m["c_dk"] = f(inp["cache_diff_k"][0, sl]).reshape(NS, PAST, 1024)
        m["c_dv"] = f(inp["cache_diff_v"][0, sl]).reshape(NS, PAST, 1024)
        in_maps.append(m)
    ncr = DBG_CORES
    res = run_bass_kernel_spmd(nc, in_maps[:ncr], core_ids=list(range(ncr)))
    R = list(res.results) + [res.results[0]] * (NCORES - ncr)

    def gp(name, shape):
        return np.stack([np.asarray(R[c][name], dtype=np.float32) for c in range(NCORES)]).reshape(shape)

    def gs(name, shape):
        return np.concatenate([np.asarray(R[c][name], dtype=np.float32).reshape(NS, DEC_SEQ, -1)
                               for c in range(NCORES)]).reshape(shape)

    global DBG_OUT
    DBG_OUT = [np.asarray(R[0][f"dbg{i}"]) for i in range(4)] if DBG_STEP == 3 else []
    B, SB = NCORES, NCORES * NS
    return (
        gp("y_p", (B, SEQ, D)), gs("y_s", (SB, DEC_SEQ, D)),
        gp("o_ckv_p", (1, B, SEQ, 256)), gp("o_krope_p", (1, B, SEQ, 32)),
        gp("o_fk_p", (1, B, SEQ, 8, 64)), gp("o_fv_p", (1, B, SEQ, 8, 64)), gp("o_logf_p", (1, B, SEQ, 8)),
        gp("o_dk_p", (1, B, SEQ, 8, 128)), gp("o_dv_p", (1, B, SEQ, 8, 128)),
        gs("o_ckv_s", (1, SB, DEC_SEQ, 256)), gs("o_krope_s", (1, SB, DEC_SEQ, 32)),
        gs("o_fk_s", (1, SB, DEC_SEQ, 8, 64)), gs("o_fv_s", (1, SB, DEC_SEQ, 8, 64)),
        gs("o_logf_s", (1, SB, DEC_SEQ, 8)),
        gs("o_dk_s", (1, SB, DEC_SEQ, 8, 128)), gs("o_dv_s", (1, SB, DEC_SEQ, 8, 128)),
    )


LAM_INIT = 0.8 - 0.6 * math.exp(-0.3 * 1)


def phase_attn(K, layer):
    nc, S, X, SC = K.nc, K.S, K.X, K.SC
    with ExitStack() as ph:
        mk = lambda name, shape, dt, psum=False: T(K, ph, name, shape, dt, psum)
        qT = [mk(f"aqT{i}", [128, SEQ], BF16) for i in range(2)]
        qT2 = [mk(f"aqTb{i}", [128, SEQ], BF16) for i in range(2)]
        kT = [mk(f"akT{i}", [128, SEQ], BF16) for i in range(2)]
        Vt = [mk(f"aV{i}", [128, 64, 192], BF16) for i in range(2)]
        rl2 = mk("arl2", [128, 512], F32)
        pT = [mk(f"apT{i}", [128, 512], BF16) for i in range(6)]
        rl = mk("arl", [128, 512], F32)
        ona, onb, comb, rr = (mk("aona", [128, 512], F32), mk("aonb", [128, 512], F32), mk("acomb", [128, 512], F32),
                              mk("arr", [128, 512], F32))
        sq = mk("asq", [128, 512], BF16)
        ob = [mk(f"aob{i}", [128, 512], BF16) for i in range(2)]
        cum_all = mk("acum", [128, 64, 8], F32)
        cs_all = mk("acs", [128, 64, 8], F32)
        biasq = [mk(f"abq{i}", [128, 4, 64], F32) for i in range(2)]
        NSP = 6 if layer == 0 else 4
        sp = [mk(f"asp{i}", [128, 512], F32, True) for i in range(NSP)]
        po = [mk(f"apo{i}", [128, 512], F32, True) for i in range(2)]
        pl = [mk(f"apl{i}", [128, 512], F32, True) for i in range(2)] if layer == 1 else [None, None]
        pm = None
        neglam, gs = mk("aneglam", [128, 1], F32), mk("ags", [128, 1], F32)
        S.op("gpsimd", lambda e: e.memset(cum_all.t[:], 0.0), writes=[cum_all.b])
        S.op("gpsimd", lambda e: e.memset(cs_all.t[:], 0.0), writes=[cs_all.b])
        if layer == 0:
            for vt_ in Vt:
                S.op("gpsimd", lambda e, vt_=vt_: e.memset(vt_.t[:, :, 64:128], 1.0), writes=[vt_.b])
        cnt = Ctx()
        cnt.task = 0
        cnt.grp = 0
        cnt.ob = 0

        if layer == 1:
            lam_t, pr, sm = mk("alam", [1, 256], F32), mk("apr", [1, 128], F32), mk("asm", [1, 2], F32)
            lscr = nc.dram_tensor(f"lam_scr{layer}", [1, 1], F32, kind="Internal")
            S.dma("sync", lambda e: e.dma_start(out=lam_t.t[:], in_=dap(X.c_lam, 0, 256, 1, 256)), writes=[lam_t.b],
                  dsem=lam_t.ds)
            S.op("vector", lambda e: e.tensor_tensor(out=pr.t[:, 0:64], in0=lam_t.t[:, 0:64], in1=lam_t.t[:, 64:128],
                                                     op=ALU.mult), reads=[lam_t.b], writes=[pr.b])
            S.op("vector", lambda e: e.tensor_tensor(out=pr.t[:, 64:128], in0=lam_t.t[:, 128:192],
                                                     in1=lam_t.t[:, 192:256], op=ALU.mult), reads=[lam_t.b],
                 writes=[pr.b])
            S.op("vector", lambda e: e.tensor_reduce(out=sm.t[:, 0:2], in_=pr.t[:].rearrange("p (a b) -> p a b", b=64),
                                                     axis=mybir.AxisListType.X, op=ALU.add), reads=[pr.b],
                 writes=[sm.b])
            S.op("scalar", lambda e: e.activation(out=sm.t[:], in_=sm.t[:], func=AF.Exp), reads=[sm.b], writes=[sm.b])
            S.op("vector", lambda e: e.tensor_tensor(out=pr.t[:, 0:1], in0=sm.t[:, 1:2], in1=sm.t[:, 0:1],
                                                     op=ALU.subtract), reads=[sm.b], writes=[pr.b])
            S.op("vector", lambda e: e.tensor_scalar(out=pr.t[:, 0:1], in0=pr.t[:, 0:1], scalar1=-LAM_INIT,
                                                     scalar2=None, op0=ALU.add), reads=[pr.b], writes=[pr.b])
            Bl = Buf("lamscr")
            S.dma("sync", lambda e: e.dma_start(out=lscr.ap(), in_=pr.t[:, 0:1]), reads=[pr.b], writes=[Bl],
                  dsem=pr.ds)
            S.dma("sync", lambda e: e.dma_start(out=neglam.t[:], in_=bc_rows(lscr, 0, 1)), reads=[Bl],
                  writes=[neglam.b], dsem=neglam.ds)
            S.dma("sync", lambda e: e.dma_start(out=gs.t[:], in_=dap(X.c_g_sub, 0, 1, 128, 1)), writes=[gs.b],
                  dsem=gs.ds)
            S.op("vector", lambda e: e.tensor_scalar(out=gs.t[:], in0=gs.t[:], scalar1=1.0 - LAM_INIT, scalar2=None,
                                                     op0=ALU.mult), reads=[gs.b], writes=[gs.b])

        groups = []
        for s, sq_ in enumerate(K.SEQS):
            sc = SC[s]
            if layer == 0:
                for h in range(8):
                    groups.append(dict(seq=s, qsrc=(sc.qmT, h, 96), ksrc=(sc.kmT, h, 96),
                                       vsrc=[(sc.v0, h * 64, 64, 0)],
                                       maps=[dict(pb=0, dq=96, voff=0, e=64, scale=96 ** -0.5, mode="chunk",
                                                  fox=None, row0=h * 64, aug=(0, 0, 64))], diff=False))
                for p in range(4):
                    groups.append(dict(seq=s, qsrc=(sc.qfT, p, 128), ksrc=(sc.kfT, p, 128),
                                       vsrc=[(sc.v0, 512 + p * 128, 64, 0), (sc.v0, 512 + p * 128 + 64, 64, 128)],
                                       maps=[dict(pb=64 * m, dq=64, voff=64 * m, e=64, scale=0.125, mode="causal",
                                                  fox=2 * p + m, row0=512 + (2 * p + m) * 64,
                                                  aug=((0, 0, 64), (64, 64, 0))[m]) for m in range(2)],
                                       diff=False))
            else:
                for h in range(8):
                    groups.append(dict(seq=s, qsrc=(sc.qdT, h, 128), ksrc=(sc.kdT, h, 128),
                                       vsrc=[(sc.v1, h * 128, 128, 0)],
                                       maps=[dict(pb=64 * m, dq=64, voff=0, e=128, scale=0.125, mode="chunk",
                                                  fox=None, row0=h * 128, aug=None) for m in range(2)], diff=True))

        def load_group(g, slot):
            s = g["seq"]
            info = K.SEQS[s]
            Tq, L, LP = info["Tq"], info["L"], info["LP"]
            TqP = max(Tq, 32)
            qh, qi, qr = g["qsrc"]
            kh, ki_, kr = g["ksrc"]
            qt, kt, vt = qT[slot], kT[slot], Vt[slot]
            S.dma("sync", lambda e: e.dma_start(out=qt.t[0:qr, 0:Tq], in_=dap(qh, qi * qr * TqP, TqP, qr, Tq)),
                  writes=[qt.b], dsem=qt.ds)
            S.dma("sync", lambda e: e.dma_start(out=kt.t[0:kr, 0:L], in_=dap(kh, ki_ * kr * LP, LP, kr, L)),
                  writes=[kt.b], dsem=kt.ds)
            if len(g["maps"]) == 2:
                qt2 = qT2[slot]
                S.dma("sync", lambda e: e.dma_start(out=qt2.t[0:qr, 0:Tq], in_=dap(qh, qi * qr * TqP, TqP, qr, Tq)),
                      writes=[qt2.b], dsem=qt2.ds)
                S.op("gpsimd", lambda e: e.memset(qt.t[64:128, 0:Tq], 0.0), reads=[qt.b], writes=[qt.b])
                S.op("gpsimd", lambda e: e.memset(qt2.t[0:64, 0:Tq], 0.0), reads=[qt2.b], writes=[qt2.b])
            nfull = L // 128
            rem = L - nfull * 128
            for (vh, vc0, ve, dc) in g["vsrc"]:
                S.dma("sync", lambda e, vh=vh, vc0=vc0, ve=ve, dc=dc: e.dma_start(
                    out=vt.t[:, 0:nfull, dc:dc + ve],
                    in_=bass.AP(tensor=vh, offset=vc0, ap=[[1024, 128], [128 * 1024, nfull], [1, ve]])),
                    writes=[vt.b], dsem=vt.ds)
                if rem:
                    S.dma("sync", lambda e, vh=vh, vc0=vc0, ve=ve, dc=dc: e.dma_start(
                        out=vt.t[0:rem, nfull, dc:dc + ve],
                        in_=bass.AP(tensor=vh, offset=nfull * 128 * 1024 + vc0, ap=[[1024, rem], [1, ve]])),
                        writes=[vt.b], dsem=vt.ds)

        def load_fox_tables(s):
            info = K.SEQS[s]
            L, past, Tq = info["L"], info["past"], info["Tq"]
            nfull = L // 128
            cum = SC[s].cum
            S.dma("sync", lambda e: e.dma_start(
                out=cum_all.t[:, 0:nfull, :], in_=bass.AP(tensor=cum, offset=0, ap=[[8, 128], [1024, nfull], [1, 8]])),
                writes=[cum_all.b], dsem=cum_all.ds)
            rem = L - nfull * 128
            if rem:
                S.dma("sync", lambda e: e.dma_start(
                    out=cum_all.t[0:rem, nfull, :],
                    in_=bass.AP(tensor=cum, offset=nfull * 1024, ap=[[8, rem], [1, 8]])),
                    writes=[cum_all.b], dsem=cum_all.ds)
            nqb = max(1, Tq // 128)
            if past == 0:
                S.op("gpsimd", lambda e: e.memset(cs_all.t[:, 0, :], 0.0), writes=[cs_all.b])
                S.dma("sync", lambda e: e.dma_start(
                    out=cs_all.t[:, 1:nqb, :],
                    in_=bass.AP(tensor=cum, offset=127 * 8, ap=[[0, 128], [1024, nqb - 1], [1, 8]])),
                    writes=[cs_all.b], dsem=cs_all.ds)
            else:
                S.dma("sync", lambda e: e.dma_start(
                    out=cs_all.t[:, 0, :], in_=bass.AP(tensor=cum, offset=(past - 1) * 8, ap=[[0, 128], [1, 8]])),
                    writes=[cs_all.b], dsem=cs_all.ds)

        def run_group(g, slot):
            s = g["seq"]
            info = K.SEQS[s]
            Tq, L, past = info["Tq"], info["L"], info["past"]
            TqP = max(Tq, 32)
            QW = min(512, Tq)
            nqt = Tq // QW
            qt, kt, vt = qT[slot], kT[slot], Vt[slot]
            tasks = []
            for Tt in range(nqt):
                q0 = Tt * QW
                nkt = (past + q0 + QW + 127) // 128
                for mi, m in enumerate(g["maps"]):
                    grp = cnt.grp
                    cnt.grp += 1
                    for j in range(nkt):
                        tasks.append(dict(T=Tt, q0=q0, mi=mi, m=m, j=j, nkt=nkt, grp=grp))

            def emit_qk(t, ti):
                m, j, q0 = t["m"], t["j"], t["q0"]
                kw = min(128, L - j * 128)
                c0 = max(0, j * 128 - (past + q0))
                t["kw"], t["c0"], t["sp"] = kw, c0, sp[ti % NSP]
                spt = t["sp"]
                if len(g["maps"]) == 2:
                    qsel = qt if t["mi"] == 0 else qT2[slot]
                    r0_, r1_ = 0, 128
                else:
                    qsel = qt
                    r0_, r1_ = m["pb"], m["pb"] + m["dq"]
                S.op("tensor", lambda e: e.matmul(spt.t[0:kw, c0:QW], lhsT=kt.t[r0_:r1_, j * 128:j * 128 + kw],
                                                  rhs=qsel.t[r0_:r1_, q0 + c0:q0 + QW], start=True, stop=True),
                     reads=[kt.b, qsel.b], writes=[spt.b])

            def emit_rest(t, ti):
                m, j, q0, nkt, grp = t["m"], t["j"], t["q0"], t["nkt"], t["grp"]
                kw, c0, spt = t["kw"], t["c0"], t["sp"]
                e_ = m["e"]
                ptt = pT[ti % len(pT)]
                pot, plt = po[grp % 2], pl[grp % 2]
                diag = j * 128 >= past + q0
                bw = min(128, QW - c0)
                if m["fox"] is not None:
                    hh = m["fox"]
                    bq = biasq[grp % 2]
                    nblk = 1
                    if j == 0:
                        for b in range(nblk):
                            qb = (q0 // 128) + (2 if QW == 512 else 0)
                            S.op("vector", lambda e, b=b, qb=qb: e.tensor_scalar(
                                out=bq.t[:, b, 0:nkt], in0=cum_all.t[:, 0:nkt, hh], scalar1=-1.0,
                                scalar2=cs_all.t[:, qb, hh:hh + 1], op0=ALU.mult, op1=ALU.add),
                                reads=[cum_all.b, cs_all.b], writes=[bq.b])
                    for b in range(1):
                        a0, a1 = c0, QW
                        S.op("scalar", lambda e, b=b, a0=a0, a1=a1: e.activation(
                            out=ptt.t[0:kw, a0:a1], in_=spt.t[0:kw, a0:a1], func=AF.Exp, scale=m["scale"],
                            bias=bq.t[0:kw, b, j:j + 1]), reads=[spt.b, bq.b], writes=[ptt.b])
                else:
                    S.op("scalar", lambda e: e.activation(out=ptt.t[0:kw, c0:QW], in_=spt.t[0:kw, c0:QW], func=AF.Exp,
                                                          scale=m["scale"]), reads=[spt.b], writes=[ptt.b])
                if diag:
                    if m["mode"] == "chunk":
                        if kw > 64:
                            S.op("gpsimd", lambda e: e.memset(ptt.t[64:kw, c0:c0 + min(64, bw)], 0.0),
                                 reads=[ptt.b], writes=[ptt.b])
                    else:
                        S.op("gpsimd", lambda e: e.affine_select(
                            out=ptt.t[0:kw, c0:c0 + bw], in_=ptt.t[0:kw, c0:c0 + bw], pattern=[[1, bw]],
                            compare_op=ALU.is_ge, fill=0.0, base=0, channel_multiplier=-1),
                            reads=[ptt.b], writes=[ptt.b])
                voff = m["voff"]
                oT_h = SC[s].oT
                if m["aug"] is not None:
                    vc, o0, l0 = m["aug"]
                    S.op("tensor", lambda e: e.matmul(pot.t[0:128, c0:QW], lhsT=vt.t[0:kw, j, vc:vc + 128],
                                                      rhs=ptt.t[0:kw, c0:QW], start=(j == 0), stop=(j == nkt - 1)),
                         reads=[vt.b, ptt.b], writes=[pot.b])
                    if j != nkt - 1:
                        return
                    S.op("vector", lambda e: e.reciprocal(out=rl.t[l0:l0 + 64, 0:QW], in_=pot.t[l0:l0 + 64, 0:QW]),
                         reads=[pot.b], writes=[rl.b])
                    S.dma("sync", lambda e: e.dma_start(out=rl2.t[o0:o0 + 64, 0:QW], in_=rl.t[l0:l0 + 64, 0:QW]),
                          reads=[rl.b], writes=[rl2.b], dsem=rl2.ds)
                    obt = ob[cnt.ob % 2]
                    cnt.ob += 1
                    S.op("vector", lambda e: e.tensor_tensor(out=obt.t[o0:o0 + 64, 0:QW], in0=pot.t[o0:o0 + 64, 0:QW],
                                                             in1=rl2.t[o0:o0 + 64, 0:QW], op=ALU.mult),
                         reads=[pot.b, rl2.b], writes=[obt.b])
                    store(K, dap(oT_h, m["row0"] * TqP + q0, TqP, 64, QW), obt.t[o0:o0 + 64, 0:QW], obt)
                    return
                S.op("tensor", lambda e: e.matmul(pot.t[0:e_, c0:QW], lhsT=vt.t[0:kw, j, voff:voff + e_],
                                                  rhs=ptt.t[0:kw, c0:QW], start=(j == 0), stop=(j == nkt - 1)),
                     reads=[vt.b, ptt.b], writes=[pot.b])
                S.op("tensor", lambda e: e.matmul(plt.t[0:e_, c0:QW], lhsT=K.ones_b[0:kw, 0:e_],
                                                  rhs=ptt.t[0:kw, c0:QW], start=(j == 0), stop=(j == nkt - 1)),
                     reads=[ptt.b], writes=[plt.b])
                if j != nkt - 1:
                    return
                S.op("vector", lambda e: e.reciprocal(out=rl.t[0:e_, 0:QW], in_=plt.t[0:e_, 0:QW]), reads=[plt.b],
                     writes=[rl.b])
                dst = ona if t["mi"] == 0 else onb
                S.op("vector", lambda e: e.tensor_tensor(out=dst.t[:, 0:QW], in0=pot.t[:, 0:QW], in1=rl.t[:, 0:QW],
                                                         op=ALU.mult), reads=[pot.b, rl.b], writes=[dst.b])
                if t["mi"] == 0:
                    return
                S.op("vector", lambda e: e.scalar_tensor_tensor(out=comb.t[:, 0:QW], in0=onb.t[:, 0:QW],
                                                                scalar=neglam.t[:, 0:1], in1=ona.t[:, 0:QW],
                                                                op0=ALU.mult, op1=ALU.add),
                     reads=[ona.b, onb.b, neglam.b], writes=[comb.b])
                S.op("scalar", lambda e: e.activation(out=sq.t[:, 0:QW], in_=comb.t[:, 0:QW], func=AF.Square),
                     reads=[comb.b], writes=[sq.b])
                pmt = plt
                S.op("tensor", lambda e: e.matmul(pmt.t[:, 0:QW], lhsT=K.mean_b[:, :], rhs=sq.t[:, 0:QW], start=True,
                                                  stop=True), reads=[sq.b], writes=[pmt.b])
                S.op("scalar", lambda e: e.activation(out=rr.t[:, 0:QW], in_=pmt.t[:, 0:QW], func=AF.Ln,
                                                      bias=K.eps5[:, :]), reads=[pmt.b], writes=[rr.b])
                S.op("scalar", lambda e: e.activation(out=rr.t[:, 0:QW], in_=rr.t[:, 0:QW], func=AF.Exp, scale=-0.5),
                     reads=[rr.b], writes=[rr.b])
                obt = ob[cnt.ob % 2]
                cnt.ob += 1
                S.op("vector", lambda e: e.scalar_tensor_tensor(out=obt.t[:, 0:QW], in0=comb.t[:, 0:QW],
                                                                scalar=gs.t[:, 0:1], in1=rr.t[:, 0:QW],
                                                                op0=ALU.mult, op1=ALU.mult),
                     reads=[comb.b, gs.b, rr.b], writes=[obt.b])
                store(K, dap(oT_h, m["row0"] * TqP + q0, TqP, 128, QW), obt.t[:, 0:QW], obt)

            LA = NSP - 1
            base = cnt.task
            for i in range(len(tasks) + LA):
                if i < len(tasks):
                    emit_qk(tasks[i], base + i)
                if i >= LA:
                    emit_rest(tasks[i - LA], base + i - LA)
            cnt.task += len(tasks)

        gl = groups if DBG_GROUPS is None else [groups[i] for i in DBG_GROUPS]
        cur_fox_seq = None
        if gl:
            load_group(gl[0], 0)
        for gi, g in enumerate(gl):
            if gi + 1 < len(gl):
                load_group(gl[gi + 1], (gi + 1) % 2)
            if g["maps"][0]["fox"] is not None and cur_fox_seq != g["seq"]:
                load_fox_tables(g["seq"])
                cur_fox_seq = g["seq"]
            run_group(g, gi % 2)
        S.barrier()
        S.emit()


def phase_mlp(K, layer):
    nc, S, X, O, SC = K.nc, K.S, K.X, K.O, K.SC
    w_o_h = X.a_w_out if layer == 0 else X.c_w_out
    with ExitStack() as ph:
        mk = lambda name, shape, dt, psum=False: T(K, ph, name, shape, dt, psum)
        w_out = mk("mw_out", [128, 8, D], BF16)
        w_up = mk("mw_up", [128, 8, D_FF], BF16)
        w_dn = mk("mw_dn", [128, 32, D], BF16)
        gm = mk("mgm", [128, D], F32)
        gf = mk("mgf", [128, D], F32) if layer == 1 else None
        stg = [mk(f"mst{i}", [128, 512], F32) for i in range(2)]
        Bw = [Buf("mw0"), Buf("mw1")]
        load_w(K, w_o_h, 0, D, 8, D, w_out, stg, Bw)
        load_w(K, X.w_up, layer * D * D_FF, D_FF, 8, D_FF, w_up, stg, Bw)
        load_w(K, X.w_down, layer * D_FF * D, D, 32, D, w_dn, stg, Bw)
        load_bc(K, X.g_mlp, layer * D, D, gm)
        WB = Bw + [gm.b]
        if layer == 1:
            load_bc(K, X.g_final, 0, D, gf)
            WB.append(gf.b)
        oTt = mk("moT", [128, 8, 256], BF16)
        xt = [mk(f"mx{i}", [128, D], F32) for i in range(2)]
        ss, rs = mk("mss", [128, 1], F32), mk("mrs", [128, 1], F32)
        xn = mk("mxn", [128, D], BF16)
        h2T = mk("mh2T", [128, 8, 256], BF16)
        uT = mk("muT", [128, 32, 256], BF16)
        rt = mk("mrt", [128, 256], F32)
        pmix = [mk(f"mpmix{i}", [128, 512], F32, True) for i in range(2)]
        ptr = mk("mptr", [128, 1024], BF16, True)
        pu = [mk(f"mpu{i}", [128, 512], F32, True) for i in range(2)]
        pd = [mk(f"mpd{i}", [128, 512], F32, True) for i in range(2)]

        PL = K.SEQS[0]["Tq"]
        macros = []
        xsrc_p = X.xp if layer == 0 else K.x1[0]
        xsrc_s = X.xs if layer == 0 else K.x1[1]
        for t0 in range(0, PL, 256):
            macros.append(dict(
                subs=[dict(n=128, xsrc=dap(xsrc_p, (t0 + i * 128) * D, D, 128, D),
                           dst=dap(K.x1[0] if layer == 0 else O.y[0], (t0 + i * 128) * D, D, 128, D)) for i in range(2)],
                oloads=[(SC[0].oT, max(PL, 32), t0, 256, 0)]))
        macros.append(dict(
            subs=[dict(n=NS * DEC_SEQ, xsrc=dap(xsrc_s, 0, D, NS * DEC_SEQ, D),
                       dst=dap(K.x1[1] if layer == 0 else O.y[1], 0, D, NS * DEC_SEQ, D))],
            oloads=[(SC[1 + s].oT, DEC_SEQ, 0, DEC_SEQ, s * DEC_SEQ) for s in range(NS)]))

        for mc in macros:
            for (oh, TqP, c0, ncols, d0) in mc["oloads"]:
                S.dma("sync", lambda e, oh=oh, TqP=TqP, c0=c0, ncols=ncols, d0=d0: e.dma_start(
                    out=oTt.t[:, :, d0:d0 + ncols],
                    in_=bass.AP(tensor=oh, offset=c0, ap=[[TqP, 128], [128 * TqP, 8], [1, ncols]])),
                    writes=[oTt.b], dsem=oTt.ds)
            ntok = sum(sbt["n"] for sbt in mc["subs"])
            for si, sbt in enumerate(mc["subs"]):
                n = sbt["n"]
                c_lo = si * 128
                x_ = xt[si]
                S.dma("sync", lambda e, x_=x_, sbt=sbt, n=n: e.dma_start(out=x_.t[0:n, :], in_=sbt["xsrc"]),
                      writes=[x_.b], dsem=x_.ds)
                for half in range(2):
                    for c in range(8):
                        S.op("tensor", lambda e, half=half, c=c, n=n, c_lo=c_lo: e.matmul(
                            pmix[half].t[0:n, :], lhsT=oTt.t[:, c, c_lo:c_lo + n],
                            rhs=w_out.t[:, c, half * 512:(half + 1) * 512], start=(c == 0), stop=(c == 7)),
                            reads=[oTt.b] + WB, writes=[pmix[half].b])
                for half in range(2):
                    S.op("vector", lambda e, half=half, n=n, x_=x_: e.tensor_tensor(
                        out=x_.t[0:n, half * 512:(half + 1) * 512], in0=x_.t[0:n, half * 512:(half + 1) * 512],
                        in1=pmix[half].t[0:n, :], op=ALU.add), reads=[x_.b, pmix[half].b], writes=[x_.b])
                rms_stats(K, x_.t[0:n, :], n, D, 1e-6, xn, ss, rs, [x_.b])
                S.op("vector", lambda e, n=n, x_=x_: e.scalar_tensor_tensor(
                    out=xn.t[0:n, :], in0=x_.t[0:n, :], scalar=rs.t[0:n, 0:1], in1=gm.t[0:n, :], op0=ALU.mult,
                    op1=ALU.mult), reads=[x_.b, rs.b, gm.b], writes=[xn.b])
                pv = ptr.t[:].rearrange("p (b c) -> p b c", c=128)
                for c in range(8):
                    S.op("tensor", lambda e, c=c, n=n: e.transpose(out=pv[:, c, 0:n], in_=xn.t[0:n, c * 128:(c + 1) * 128],
                                                                   identity=K.ident[0:n, 0:n]),
                         reads=[xn.b], writes=[ptr.b])
                S.op("scalar", lambda e, n=n, c_lo=c_lo: e.activation(out=h2T.t[:, :, c_lo:c_lo + n], in_=pv[:, :, 0:n],
                                                                      func=AF.Copy), reads=[ptr.b], writes=[h2T.b])
            for fc in range(32):
                put = pu[fc % 2]
                for c in range(8):
                    S.op("tensor", lambda e, fc=fc, c=c, put=put, ntok=ntok: e.matmul(
                        put.t[:, 0:ntok], lhsT=w_up.t[:, c, fc * 128:(fc + 1) * 128], rhs=h2T.t[:, c, 0:ntok],
                        start=(c == 0), stop=(c == 7)), reads=[h2T.b] + WB, writes=[put.b])
                S.op("scalar", lambda e, put=put, ntok=ntok: e.activation(out=rt.t[:, 0:ntok], in_=put.t[:, 0:ntok], func=AF.Relu),
                     reads=[put.b], writes=[rt.b])
                S.op("vector", lambda e, fc=fc, put=put, ntok=ntok: e.tensor_tensor(out=uT.t[:, fc, 0:ntok], in0=rt.t[:, 0:ntok],
                                                                         in1=put.t[:, 0:ntok], op=ALU.mult),
                     reads=[rt.b, put.b], writes=[uT.b])
            for si, sbt in enumerate(mc["subs"]):
                n = sbt["n"]
                c_lo = si * 128
                x_ = xt[si]
                for half in range(2):
                    for fc in range(32):
                        S.op("tensor", lambda e, half=half, fc=fc, n=n, c_lo=c_lo: e.matmul(
                            pd[half].t[0:n, :], lhsT=uT.t[:, fc, c_lo:c_lo + n],
                            rhs=w_dn.t[:, fc, half * 512:(half + 1) * 512], start=(fc == 0), stop=(fc == 31)),
                            reads=[uT.b] + WB, writes=[pd[half].b])
                for half in range(2):
                    S.op("vector", lambda e, half=half, n=n, x_=x_: e.tensor_tensor(
                        out=x_.t[0:n, half * 512:(half + 1) * 512], in0=x_.t[0:n, half * 512:(half + 1) * 512],
                        in1=pd[half].t[0:n, :], op=ALU.add), reads=[x_.b, pd[half].b], writes=[x_.b])
                if layer == 0:
                    store(K, sbt["dst"], x_.t[0:n, :], x_)
                else:
                    rms_stats(K, x_.t[0:n, :], n, D, 1e-6, xn, ss, rs, [x_.b])
                    S.op("vector", lambda e, n=n, x_=x_: e.scalar_tensor_tensor(
                        out=x_.t[0:n, :], in0=x_.t[0:n, :], scalar=rs.t[0:n, 0:1], in1=gf.t[0:n, :], op0=ALU.mult,
                        op1=ALU.mult), reads=[x_.b, rs.b, gf.b], writes=[x_.b])
                    store(K, sbt["dst"], x_.t[0:n, :], x_)
        S.barrier()
        S.emit()


def phase_proj1(K):
    nc, S, X, O, SC = K.nc, K.S, K.X, K.O, K.SC
    with ExitStack() as ph:
        mk = lambda name, shape, dt, psum=False: T(K, ph, name, shape, dt, psum)
        K.ropeb = Buf("ropetmp1")
        w_c = mk("cw_in", [128, 8, 3072], BF16)
        gmix = mk("cgmix", [128, D], F32)
        stg = [mk(f"cst{i}", [128, 1024], F32) for i in range(2)]
        Bw = [Buf("cw0"), Buf("cw1")]
        load_w(K, X.c_w_in, 0, 3072, 8, 3072, w_c, stg, Bw)
        load_bc(K, X.g_mix, D, D, gmix)
        WB = Bw + [gmix.b]
        xt = [mk(f"cx{i}", [128, D], F32) for i in range(2)]
        junk = mk("cjunk", [128, D], BF16)
        ss, rs = mk("css", [128, 1], F32), mk("crs", [128, 1], F32)
        xn = mk("cxn", [128, D], BF16)
        hT = mk("chT", [128, 8, 128], BF16)
        t1, t2 = mk("ct1", [128, 8, 8], F32), mk("ct2", [128, 8, 8], F32)
        qd_b, kd_b, v_b = mk("cqd_b", [128, 16, 64], BF16), mk("ckd_b", [128, 16, 64], BF16), mk("cv_b", [128, D], BF16)
        k_f, v_f = mk("ck_f", [128, 16, 64], F32), mk("cv_f", [128, D], F32)
        qdT_s, kdT_s = mk("cqdT_s", [128, 8, 128], BF16), mk("ckdT_s", [128, 8, 128], BF16)
        pb = [mk(f"cpb{i}", [128, 512], F32, True) for i in range(6)]
        pt = [mk(f"cpt{i}", [128, 1024], BF16, True) for i in range(2)]

        def kside(n, seq, kpos0):
            sc = SC[seq]
            LP = K.SEQS[seq]["LP"]
            transpose_blocks(K, kd_b, n, [(b * 128, 128) for b in range(8)], pt[1], kdT_s, "vector")
            store(K, bass.AP(tensor=sc.kdT, offset=kpos0, ap=[[LP, 128], [128 * LP, 8], [1, n]]),
                  kdT_s.t[:, :, 0:n], kdT_s)
            store(K, dap(sc.v1, kpos0 * 1024, 1024, n, 1024), v_b.t[0:n, :], v_b)

        kf2 = k_f.t[:].rearrange("p h c -> p (h c)")
        kb2 = kd_b.t[:].rearrange("p h c -> p (h c)")
        for s in range(NS):
            for i in range(PAST // 128 if DBG_CACHE is None else DBG_CACHE):
                r0 = (s * PAST + i * 128)
                S.dma("sync", lambda e, r0=r0: e.dma_start(out=kf2[:, :], in_=dap(X.c_dk, r0 * 1024, 1024, 128, 1024)),
                      writes=[k_f.b], dsem=k_f.ds)
                S.op("gpsimd", lambda e: e.tensor_copy(out=kb2[:, :], in_=kf2[:, :]), reads=[k_f.b], writes=[kd_b.b])
                S.dma("sync", lambda e, r0=r0: e.dma_start(out=v_f.t[:], in_=dap(X.c_dv, r0 * 1024, 1024, 128, 1024)),
                      writes=[v_f.b], dsem=v_f.ds)
                S.op("vector", lambda e: e.tensor_copy(out=v_b.t[:], in_=v_f.t[:]), reads=[v_f.b], writes=[v_b.b])
                kside(128, 1 + s, i * 128)

        tiles = token_tiles(K)
        xh = [K.x1[0], K.x1[1]]
        for it, tl in enumerate(tiles):
            n, seq, pos0, row0, oi = tl["n"], tl["seq"], tl["pos0"], tl["row0"], tl["oi"]
            xb = xt[it % 2]
            S.dma("sync", lambda e, xb=xb, tl=tl, n=n: e.dma_start(out=xb.t[0:n, :],
                                                                   in_=dap(xh[tl["xi"]], tl["row0"] * D, D, n, D)),
                  writes=[xb.b], dsem=xb.ds)
            rms_stats(K, xb.t[0:n, :], n, D, 1e-6, junk, ss, rs, [xb.b])
            S.op("vector", lambda e, xb=xb, n=n: e.scalar_tensor_tensor(out=xn.t[0:n, :], in0=xb.t[0:n, :],
                                                                        scalar=rs.t[0:n, 0:1], in1=gmix.t[0:n, :],
                                                                        op0=ALU.mult, op1=ALU.mult),
                 reads=[xb.b, rs.b, gmix.b], writes=[xn.b])
            transpose_blocks(K, xn, n, [(c * 128, 128) for c in range(8)], pt[0], hT, "scalar")
            for g6 in range(6):
                for c in range(8):
                    S.op("tensor", lambda e, g6=g6, c=c, n=n: e.matmul(
                        pb[g6].t[0:n, :], lhsT=hT.t[:, c, 0:n], rhs=w_c.t[:, c, g6 * 512:(g6 + 1) * 512],
                        start=(c == 0), stop=(c == 7)), reads=[hT.b] + WB, writes=[pb[g6].b])
            cos, sin = rope_tables(K, tl, n)
            cos8 = bass.AP(tensor=cos.tensor, offset=cos.offset, ap=[list(cos.ap[0]), [0, 8], [2, 8]])
            sin8 = bass.AP(tensor=sin.tensor, offset=sin.offset, ap=[list(sin.ap[0]), [0, 8], [2, 8]])
            for b2 in range(2):
                pq = pb[b2].t[:].rearrange("p (h c) -> p h c", c=64)
                rope(K, pq[0:n, :, 0:8], pq[0:n, :, 8:16], qd_b.t[0:n, b2 * 8:(b2 + 1) * 8, 0:8],
                     qd_b.t[0:n, b2 * 8:(b2 + 1) * 8, 8:16], cos8, sin8, t1.t[0:n, :, :], t2.t[0:n, :, :],
                     [pb[b2].b], [qd_b.b])
                S.op("vector", lambda e, b2=b2, pq=pq, n=n: e.tensor_copy(out=qd_b.t[0:n, b2 * 8:(b2 + 1) * 8, 16:64],
                                                                         in_=pq[0:n, :, 16:64]),
                     reads=[pb[b2].b], writes=[qd_b.b])
                pk = pb[2 + b2].t[:].rearrange("p (h c) -> p h c", c=64)
                rope(K, pk[0:n, :, 0:8], pk[0:n, :, 8:16], k_f.t[0:n, b2 * 8:(b2 + 1) * 8, 0:8],
                     k_f.t[0:n, b2 * 8:(b2 + 1) * 8, 8:16], cos8, sin8, t1.t[0:n, :, :], t2.t[0:n, :, :],
                     [pb[2 + b2].b], [k_f.b])
                S.op("vector", lambda e, b2=b2, pk=pk, n=n: e.tensor_copy(out=k_f.t[0:n, b2 * 8:(b2 + 1) * 8, 16:64],
                                                                         in_=pk[0:n, :, 16:64]),
                     reads=[pb[2 + b2].b], writes=[k_f.b])
                S.op("scalar", lambda e, b2=b2, n=n: e.activation(out=v_f.t[0:n, b2 * 512:(b2 + 1) * 512],
                                                                  in_=pb[4 + b2].t[0:n, :], func=AF.Copy),
                     reads=[pb[4 + b2].b], writes=[v_f.b])
            S.op("gpsimd", lambda e, n=n: e.tensor_copy(out=kb2[0:n, :], in_=kf2[0:n, :]), reads=[k_f.b],
                 writes=[kd_b.b])
            S.op("gpsimd", lambda e, n=n: e.tensor_copy(out=v_b.t[0:n, :], in_=v_f.t[0:n, :]), reads=[v_f.b],
                 writes=[v_b.b])
            store(K, dap(O.dk[oi], row0 * 1024, 1024, n, 1024), kf2[0:n, :], k_f)
            store(K, dap(O.dv[oi], row0 * 1024, 1024, n, 1024), v_f.t[0:n, :], v_f)
            sc = SC[seq]
            TqP = max(K.SEQS[seq]["Tq"], 32)
            qcol0 = pos0 - K.SEQS[seq]["past"]
            transpose_blocks(K, qd_b, n, [(b * 128, 128) for b in range(8)], pt[0], qdT_s, "vector")
            store(K, bass.AP(tensor=sc.qdT, offset=qcol0, ap=[[TqP, 128], [128 * TqP, 8], [1, n]]),
                  qdT_s.t[:, :, 0:n], qdT_s)
            kside(n, seq, pos0)
        S.barrier()
        S.emit()
```

```python
import math
from contextlib import ExitStack
import numpy as np
import concourse.bass as bass
import concourse.mybir as mybir
from concourse.bass_utils import run_bass_kernel_spmd

F32 = mybir.dt.float32
BF16 = mybir.dt.bfloat16
I32 = mybir.dt.int32
AF = mybir.ActivationFunctionType
ALU = mybir.AluOpType

NCORES = 8
D = 1024
SEQ = 8192
DEC_SEQ = 32
PAST = 4096
NS = 2
CHUNK = 64
THETA = 500000.0
Q_LORA, KV_LORA, MLA_ROPE, MLA_NOPE, MLA_V = 512, 256, 32, 64, 64
H = 8
A_IN = 2344
D_FF = 4096
LS = PAST + DEC_SEQ
LSP = 33 * 128
STAGES = 6
DBG_TILES = None
DBG_CACHE = None
DBG_STEP = 99
KV_STEP = 99
KV_SUB = 99
DBG_GROUPS = None
DBG_CORES = NCORES if False else 8

ENGINES = ("tensor", "vector", "scalar", "gpsimd", "sync")


class Buf:
    __slots__ = ("name", "w", "r")

    def __init__(self, name=""):
        self.name = name
        self.w = None
        self.r = []


class Sched:
    def __init__(self, nc, stack, n_dma_sems=56):
        self.nc = nc
        self.n_dma_sems = n_dma_sems
        self.esem = {e: stack.enter_context(nc.semaphore(f"es_{e}")) for e in ENGINES}
        self.dsem = [stack.enter_context(nc.semaphore(f"ds_{i}")) for i in range(n_dma_sems)]
        self.base = {e: 0 for e in ENGINES}
        self.dma_cnt = [0] * n_dma_sems
        self._reset()

    def _reset(self):
        self.ops = {e: [] for e in ENGINES}
        self.known = {e: {} for e in ENGINES}
        self.flag = {e: set() for e in ENGINES}

    def _need(self, eng, ev, waits):
        if ev is None:
            return
        kind, key, idx = ev
        if kind == "e" and key == eng and eng == "tensor":
            return
        k = (kind, key)
        if self.known[eng].get(k, -1) >= idx:
            return
        self.known[eng][k] = idx
        waits.append(ev)
        if kind == "e":
            self.flag[key].add(idx)

    def _deps(self, eng, reads, writes):
        waits = []
        for b in reads:
            self._need(eng, b.w, waits)
        for b in writes:
            self._need(eng, b.w, waits)
            for ev in b.r:
                self._need(eng, ev, waits)
        return waits

    def _mark(self, ev, reads, writes):
        for b in writes:
            b.w = ev
            b.r = []
        for b in reads:
            if b not in writes:
                b.r.append(ev)

    def op(self, eng, fn, reads=(), writes=()):
        waits = self._deps(eng, reads, writes)
        idx = len(self.ops[eng])
        self.ops[eng].append(dict(waits=waits, fn=fn, kind="op"))
        ev = ("e", eng, idx)
        self._mark(ev, reads, writes)
        return ev

    def dma(self, eng, fn, reads=(), writes=(), dsem=0):
        waits = self._deps(eng, reads, writes)
        self.dma_cnt[dsem] += 1
        self.ops[eng].append(dict(waits=waits, fn=fn, kind="dma", dsem=dsem))
        ev = ("d", dsem, self.dma_cnt[dsem])
        self._mark(ev, reads, writes)
        return ev

    def barrier(self):
        last = {}
        for e in ENGINES:
            n = len(self.ops[e])
            idx = None
            for i in range(n - 1, -1, -1):
                if self.ops[e][i]["kind"] == "op":
                    idx = i
                    break
            last[e] = idx
        for e in ENGINES:
            waits = []
            for f in ENGINES:
                if last[f] is not None:
                    self._need(e, ("e", f, last[f]), waits) if not (f == e == "tensor") else None
            for d in range(self.n_dma_sems):
                if self.dma_cnt[d]:
                    self._need(e, ("d", d, self.dma_cnt[d]), waits)
            self.ops[e].append(dict(waits=waits, fn=None, kind="wait"))

    def emit(self):
        nc = self.nc
        val = {}
        for e in ENGINES:
            fl = sorted(self.flag[e])
            val[e] = {idx: self.base[e] + i + 1 for i, idx in enumerate(fl)}
        sched = self
        with nc.Block() as block:
            def run(engname, engobj):
                for idx, o in enumerate(sched.ops[engname]):
                    for (kind, key, n) in o["waits"]:
                        if kind == "e":
                            engobj.wait_ge(sched.esem[key], val[key][n])
                        else:
                            engobj.wait_ge(sched.dsem[key], 16 * n)
                    if o["kind"] == "op":
                        ins = o["fn"](engobj)
                        if idx in sched.flag[engname]:
                            ins.then_inc(sched.esem[engname], 1)
                    elif o["kind"] == "dma":
                        o["fn"](engobj).then_inc(sched.dsem[o["dsem"]], 16)

            @block.tensor
            def _(e):
                run("tensor", e)

            @block.vector
            def _(e):
                run("vector", e)

            @block.scalar
            def _(e):
                run("scalar", e)

            @block.gpsimd
            def _(e):
                run("gpsimd", e)

            @block.sync
            def _(e):
                run("sync", e)
        for e in ENGINES:
            self.base[e] += len(self.flag[e])
        self._reset()


class Ctx:
    pass


def sb(ph, nc, name, shape, dt):
    return ph.enter_context(nc.sbuf_tensor(name, shape, dt))


def ps(ph, nc, name, shape, dt):
    return ph.enter_context(nc.psum_tensor(name, shape, dt))


def bc_rows(handle, offset, ncols, nparts=128):
    return bass.AP(tensor=handle, offset=offset, ap=[[0, nparts], [1, ncols]])


class Rot:
    def __init__(self, tiles, name):
        self.t = tiles
        self.b = [Buf(f"{name}{i}") for i in range(len(tiles))]
        self.i = -1

    def next(self):
        self.i = (self.i + 1) % len(self.t)
        return self.t[self.i], self.b[self.i]


def build_program():
    nc = bass.Bass("TRN2", target_bir_lowering=False)
    dt_in = {}

    def din(name, shape):
        h = nc.dram_tensor(name, list(shape), F32, kind="ExternalInput")
        dt_in[name] = h
        return h

    def dout(name, shape):
        return nc.dram_tensor(name, list(shape), F32, kind="ExternalOutput")

    def dscr(name, shape, dt):
        return nc.dram_tensor(name, list(shape), dt, kind="Internal")

    X = Ctx()
    X.xp = din("xp", [SEQ, D])
    X.xs = din("xs", [NS * DEC_SEQ, D])
    X.c_ckv = din("c_ckv", [NS, PAST, KV_LORA])
    X.c_krope = din("c_krope", [NS, PAST, MLA_ROPE])
    X.c_fk = din("c_fk", [NS, PAST, 512])
    X.c_fv = din("c_fv", [NS, PAST, 512])
    X.c_logf = din("c_logf", [NS, PAST, 8])
    X.c_dk = din("c_dk", [NS, PAST, 1024])
    X.c_dv = din("c_dv", [NS, PAST, 1024])
    X.g_mix = din("g_mix", [2, D])
    X.a_w_in = din("a_w_in", [D, A_IN])
    X.a_g_q = din("a_g_q", [1, Q_LORA])
    X.a_w_uq = din("a_w_uq", [Q_LORA, 768])
    X.a_g_kv = din("a_g_kv", [1, KV_LORA])
    X.a_w_ukv = din("a_w_ukv", [KV_LORA, 1024])
    X.a_b_f = din("a_b_f", [1, 8])
    X.a_w_out = din("a_w_out", [D, D])
    X.c_w_in = din("c_w_in", [D, 3072])
    X.c_lam = din("c_lam", [1, 256])
    X.c_g_sub = din("c_g_sub", [128, 1])
    X.c_w_out = din("c_w_out", [D, D])
    X.g_mlp = din("g_mlp", [2, D])
    X.w_up = din("w_up", [2, D, D_FF])
    X.w_down = din("w_down", [2, D_FF, D])
    X.g_final = din("g_final", [1, D])

    O = Ctx()
    O.y = [dout("y_p", [SEQ, D]), dout("y_s", [NS * DEC_SEQ, D])]
    O.ckv = [dout("o_ckv_p", [SEQ, 256]), dout("o_ckv_s", [64, 256])]
    O.krope = [dout("o_krope_p", [SEQ, 32]), dout("o_krope_s", [64, 32])]
    O.fk = [dout("o_fk_p", [SEQ, 512]), dout("o_fk_s", [64, 512])]
    O.fv = [dout("o_fv_p", [SEQ, 512]), dout("o_fv_s", [64, 512])]
    O.logf = [dout("o_logf_p", [SEQ, 8]), dout("o_logf_s", [64, 8])]
    O.dk = [dout("o_dk_p", [SEQ, 1024]), dout("o_dk_s", [64, 1024])]
    O.dv = [dout("o_dv_p", [SEQ, 1024]), dout("o_dv_s", [64, 1024])]

    O.dbg = [dout(f"dbg{i}", [128, 1024]) for i in range(4)] if DBG_STEP == 3 else []
    PL = SEQ if DBG_TILES is None else DBG_TILES * 128
    SEQS = [dict(Tq=PL, L=PL, LP=SEQ, past=0), dict(Tq=DEC_SEQ, L=LS, LP=LSP, past=PAST),
            dict(Tq=DEC_SEQ, L=LS, LP=LSP, past=PAST)]
    SC = []
    for s, q in enumerate(SEQS):
        c = Ctx()
        TqP = max(q["Tq"], 32)
        c.qmT = dscr(f"qmT{s}", [8, 96, TqP], BF16)
        c.kmT = dscr(f"kmT{s}", [8, 96, q["LP"]], BF16)
        c.v0 = dscr(f"v0_{s}", [q["LP"], 1024], BF16)
        c.qfT = dscr(f"qfT{s}", [4, 128, TqP], BF16)
        c.kfT = dscr(f"kfT{s}", [4, 128, q["LP"]], BF16)
        c.cum = dscr(f"cum{s}", [q["LP"], 8], F32)
        c.qdT = dscr(f"qdT{s}", [8, 128, TqP], BF16)
        c.kdT = dscr(f"kdT{s}", [8, 128, q["LP"]], BF16)
        c.v1 = dscr(f"v1_{s}", [q["LP"], 1024], BF16)
        c.oT = dscr(f"oT{s}", [1024, TqP], BF16)
        SC.append(c)
    x1p = dscr("x1p", [SEQ, D], F32)
    x1s = dscr("x1s", [NS * DEC_SEQ, D], F32)

    top = ExitStack()
    with top:
        S = Sched(nc, top)
        ident_f = sb(top, nc, "ident_f", [128, 128], F32)
        ident = sb(top, nc, "ident", [128, 128], BF16)
        tri = sb(top, nc, "tri", [128, 128], F32)
        tri_s = sb(top, nc, "tri_s", [64, 64], F32)
        ones_f = sb(top, nc, "ones_f", [128, 128], F32)
        ones_b = sb(top, nc, "ones_b", [128, 128], BF16)
        tri_b = sb(top, nc, "tri_b", [128, 128], BF16)
        mean_b = sb(top, nc, "mean_b", [128, 128], BF16)
        cosT = sb(top, nc, "cosT", [128, 64, 16], F32)
        sinT = sb(top, nc, "sinT", [128, 64, 16], F32)
        cosS = sb(top, nc, "cosS", [64, 16], F32)
        sinS = sb(top, nc, "sinS", [64, 16], F32)
        negpi = sb(top, nc, "negpi", [128, 1], F32)
        onec = sb(top, nc, "onec", [128, 1], F32)
        eps6 = sb(top, nc, "eps6", [128, 1], F32)
        eps5 = sb(top, nc, "eps5", [128, 1], F32)
        B_const = Buf("const")

        with ExitStack() as ph:
            posi = sb(ph, nc, "posi", [128, 64], I32)
            posf = sb(ph, nc, "posf", [128, 64], F32)
            ang = sb(ph, nc, "ang", [128, 64, 16], F32)
            ang2 = sb(ph, nc, "ang2", [128, 64, 16], F32)
            possi = sb(ph, nc, "possi", [64, 1], I32)
            possf = sb(ph, nc, "possf", [64, 1], F32)
            angs = sb(ph, nc, "angs", [64, 16], F32)
            angs2 = sb(ph, nc, "angs2", [64, 16], F32)
            Bp = Buf("p0")
            G = "gpsimd"
            S.op(G, lambda e: e.memset(ones_f[:], 1.0), writes=[Bp])
            S.op(G, lambda e: e.memset(ones_b[:], 1.0), writes=[Bp])
            S.op(G, lambda e: e.memset(mean_b[:], 1.0 / 128.0), writes=[Bp])
            S.op(G, lambda e: e.memset(negpi[:], -math.pi), writes=[Bp])
            S.op(G, lambda e: e.memset(onec[:], 1.0), writes=[Bp])
            S.op(G, lambda e: e.memset(eps6[:], 1e-6), writes=[Bp])
            S.op(G, lambda e: e.memset(eps5[:], 1e-5), writes=[Bp])
            S.op(G, lambda e: e.affine_select(out=ident_f[:], in_=ones_f[:], pattern=[[1, 128]],
                                              compare_op=ALU.is_equal, fill=0.0, base=0, channel_multiplier=-1),
                 reads=[Bp], writes=[Bp])
            S.op(G, lambda e: e.tensor_copy(out=ident[:], in_=ident_f[:]), reads=[Bp], writes=[Bp])
            S.op(G, lambda e: e.affine_select(out=tri[:], in_=ones_f[:], pattern=[[1, 128]],
                                              compare_op=ALU.is_ge, fill=0.0, base=0, channel_multiplier=-1),
                 reads=[Bp], writes=[Bp])
            S.op(G, lambda e: e.tensor_copy(out=tri_s[:], in_=tri[0:64, 0:64]), reads=[Bp], writes=[Bp])
            S.op(G, lambda e: e.tensor_copy(out=tri_b[:], in_=tri[:]), reads=[Bp], writes=[Bp])
            S.op(G, lambda e: e.memset(tri_s[0:32, 32:64], 0.0), reads=[Bp], writes=[Bp])
            S.op(G, lambda e: e.iota(posi[:], [[128, 64]], base=0, channel_multiplier=1), writes=[Bp])
            S.op(G, lambda e: e.tensor_copy(out=posf[:], in_=posi[:]), reads=[Bp], writes=[Bp])
            S.op(G, lambda e: e.iota(possi[0:32, :], [[0, 1]], base=PAST, channel_multiplier=1), writes=[Bp])
            S.op(G, lambda e: e.iota(possi[32:64, :], [[0, 1]], base=PAST, channel_multiplier=1),
                 reads=[Bp], writes=[Bp])
            S.op(G, lambda e: e.tensor_copy(out=possf[:], in_=possi[:]), reads=[Bp], writes=[Bp])
            V = "vector"
            for f in range(16):
                inv = float(np.float32(THETA) ** np.float32(-f / 16.0))
                S.op(V, lambda e, f=f, inv=inv: e.tensor_scalar(out=ang[:, :, f], in0=posf[:], scalar1=inv,
                                                                scalar2=None, op0=ALU.mult),
                     reads=[Bp], writes=[Bp])
                S.op(V, lambda e, f=f, inv=inv: e.tensor_scalar(out=angs[:, f:f + 1], in0=possf[:], scalar1=inv,
                                                                scalar2=None, op0=ALU.mult),
                     reads=[Bp], writes=[Bp])
            twopi = 2.0 * math.pi
            ki = sb(ph, nc, "ki", [128, 64, 16], I32)
            kf = sb(ph, nc, "kf", [128, 64, 16], F32)
            kis = sb(ph, nc, "kis", [64, 16], I32)
            kfs = sb(ph, nc, "kfs", [64, 16], F32)
            PI_LO = 3.1415925

            def trig(dst, a, a2, k_i, k_f, shift):
                S.op(V, lambda e: e.tensor_scalar(out=a2, in0=a, scalar1=1.0 / twopi, scalar2=shift,
                                                  op0=ALU.mult, op1=ALU.add), reads=[Bp], writes=[Bp])
                S.op(V, lambda e: e.tensor_copy(out=k_i, in_=a2), reads=[Bp], writes=[Bp])
                S.op(V, lambda e: e.tensor_copy(out=k_f, in_=k_i), reads=[Bp], writes=[Bp])
                S.op(V, lambda e: e.tensor_tensor(out=a2, in0=a2, in1=k_f, op=ALU.subtract), reads=[Bp], writes=[Bp])
                S.op(V, lambda e: e.tensor_scalar(out=a2, in0=a2, scalar1=twopi, scalar2=PI_LO,
                                                  op0=ALU.mult, op1=ALU.min), reads=[Bp], writes=[Bp])
                S.op(V, lambda e: e.tensor_scalar(out=a2, in0=a2, scalar1=-PI_LO, scalar2=None,
                                                  op0=ALU.max), reads=[Bp], writes=[Bp])
                S.op("scalar", lambda e: e.activation(out=dst, in_=a2, func=AF.Sin), reads=[Bp], writes=[Bp])

            trig(sinT[:], ang[:], ang2[:], ki[:], kf[:], 0.0)
            trig(cosT[:], ang[:], ang2[:], ki[:], kf[:], 0.25)
            trig(sinS[:], angs[:], angs2[:], kis[:], kfs[:], 0.0)
            trig(cosS[:], angs[:], angs2[:], kis[:], kfs[:], 0.25)
            S.barrier()
            S.emit()

        K = Ctx()
        K.nc, K.S, K.X, K.O, K.SC, K.SEQS = nc, S, X, O, SC, SEQS
        K.x1 = [x1p, x1s]
        K.ident, K.ident_f, K.tri, K.ones_f, K.ones_b, K.mean_b = ident, ident_f, tri, ones_f, ones_b, mean_b
        K.cosT, K.sinT, K.cosS, K.sinS, K.onec = cosT, sinT, cosS, sinS, onec
        K.eps6, K.eps5 = eps6, eps5
        K.tri_b = tri_b
        stages = STAGES
        if stages >= 1:
            phase_proj0(K)
        if stages >= 2:
            phase_attn(K, 0)
        if stages >= 3:
            phase_mlp(K, 0)
        if stages >= 4:
            phase_proj1(K)
        if stages >= 5:
            phase_attn(K, 1)
        if stages >= 6:
            phase_mlp(K, 1)
    return nc


class T:
    _next_ds = [0]

    _uid = [0]

    def __init__(self, K, ph, name, shape, dt, psum=False):
        T._uid[0] += 1
        name = f"{name}_{T._uid[0]}"
        self.t = (ps if psum else sb)(ph, K.nc, name, shape, dt)
        self.b = Buf(name)
        self._ds = None

    @property
    def ds(self):
        if self._ds is None:
            self._ds = T._next_ds[0] % 56
            T._next_ds[0] += 1
        return self._ds


def dap(handle, off, rowstride, nrows, ncols):
    return bass.AP(tensor=handle, offset=off, ap=[[rowstride, nrows], [1, ncols]])


def bcast_mid(a, n):
    return bass.AP(tensor=a.tensor, offset=a.offset, ap=[list(a.ap[0]), [0, n], list(a.ap[1])])


def load_w(K, handle, off, rowstride, nk, ncols, dst, stg, Bw):
    S = K.S
    engs = ("vector", "gpsimd")
    i = 0
    CH = stg[0].t.shape[1]
    for k in range(nk):
        for c0 in range(0, ncols, CH):
            cw = min(CH, ncols - c0)
            st = stg[i % len(stg)]
            src = dap(handle, off + k * 128 * rowstride + c0, rowstride, 128, cw)
            S.dma("sync", lambda e, st=st, src=src, cw=cw: e.dma_start(out=st.t[:, 0:cw], in_=src),
                  writes=[st.b], dsem=st.ds)
            eng = engs[i % 2]
            S.op(eng, lambda e, st=st, k=k, c0=c0, cw=cw: e.tensor_copy(out=dst.t[:, k, c0:c0 + cw],
                                                                        in_=st.t[:, 0:cw]),
                 reads=[st.b], writes=[Bw[i % 2]])
            i += 1


def load_bc(K, handle, off, ncols, dst):
    K.S.dma("sync", lambda e: e.dma_start(out=dst.t[:, 0:ncols], in_=bc_rows(handle, off, ncols)),
            writes=[dst.b], dsem=dst.ds)


def rms_stats(K, src_ap, n, W, eps, junk, ss, rs, src_bufs):
    S = K.S
    S.op("gpsimd", lambda e: e.memset(ss.t[0:n, 0:1], 0.0), writes=[ss.b])
    S.op("scalar", lambda e: e.activation(out=junk.t[0:n, 0:W], in_=src_ap, func=AF.Square,
                                          accum_out=ss.t[0:n, 0:1]),
         reads=src_bufs + [ss.b], writes=[junk.b, ss.b])
    epsc = K.eps6 if eps < 5e-6 else K.eps5
    S.op("scalar", lambda e: e.activation(out=rs.t[0:n, 0:1], in_=ss.t[0:n, 0:1], func=AF.Ln, scale=1.0 / W,
                                          bias=epsc[0:n, :]), reads=[ss.b], writes=[rs.b])
    S.op("scalar", lambda e: e.activation(out=rs.t[0:n, 0:1], in_=rs.t[0:n, 0:1], func=AF.Exp, scale=-0.5),
         reads=[rs.b], writes=[rs.b])


def transpose_blocks(K, src, n, blocks, ptile, dst, evac_eng):
    S = K.S
    pv = ptile.t[:].rearrange("p (b c) -> p b c", c=128)
    src2 = src.t[:] if len(src.t.shape) == 2 else src.t[:].rearrange("p h c -> p (h c)")
    wmax = max(w for _, w in blocks)
    for b, (c0, w) in enumerate(blocks):
        S.op("tensor", lambda e, b=b, c0=c0, w=w: e.transpose(out=pv[0:w, b, 0:n], in_=src2[0:n, c0:c0 + w],
                                                             identity=K.ident[0:n, 0:n]),
             reads=[src.b], writes=[ptile.b])
    nb = len(blocks)
    if evac_eng == "scalar":
        S.op("scalar", lambda e: e.activation(out=dst.t[0:wmax, 0:nb, 0:n], in_=pv[0:wmax, 0:nb, 0:n], func=AF.Copy),
             reads=[ptile.b], writes=[dst.b])
    else:
        S.op("vector", lambda e: e.tensor_copy(out=dst.t[0:wmax, 0:nb, 0:n], in_=pv[0:wmax, 0:nb, 0:n]),
             reads=[ptile.b], writes=[dst.b])


def rope(K, x1, x2, o1, o2, cos, sin, t1, t2, rbufs, wbufs):
    S = K.S
    tb = [K.ropeb]
    V = "vector"
    S.op(V, lambda e: e.tensor_tensor(out=t1, in0=x1, in1=cos, op=ALU.mult), reads=rbufs, writes=tb)
    S.op(V, lambda e: e.tensor_tensor(out=t2, in0=x2, in1=sin, op=ALU.mult), reads=rbufs, writes=tb)
    S.op(V, lambda e: e.tensor_tensor(out=o1, in0=t1, in1=t2, op=ALU.subtract), reads=tb, writes=wbufs)
    S.op(V, lambda e: e.tensor_tensor(out=t1, in0=x2, in1=cos, op=ALU.mult), reads=rbufs + wbufs, writes=tb)
    S.op(V, lambda e: e.tensor_tensor(out=t2, in0=x1, in1=sin, op=ALU.mult), reads=rbufs, writes=tb)
    S.op(V, lambda e: e.tensor_tensor(out=o2, in0=t1, in1=t2, op=ALU.add), reads=tb, writes=wbufs)


def token_tiles(K):
    tiles = []
    for i in range(SEQ // 128 if DBG_TILES is None else DBG_TILES):
        tiles.append(dict(n=128, seq=0, pos0=i * 128, xi=0, row0=i * 128, oi=0, ti=i))
    for s in range(NS):
        tiles.append(dict(n=DEC_SEQ, seq=1 + s, pos0=PAST, xi=1, row0=s * DEC_SEQ, oi=1, ti=None))
    return tiles


def rope_tables(K, tl, n):
    if tl["ti"] is not None:
        return K.cosT[0:n, tl["ti"], :], K.sinT[0:n, tl["ti"], :]
    return K.cosS[0:n, :], K.sinS[0:n, :]


def store(K, dst_ap, src_ap, src_t):
    K.S.dma("sync", lambda e: e.dma_start(out=dst_ap, in_=src_ap), reads=[src_t.b], dsem=src_t.ds)


def _ld_x(K, tiles, xt, xh, k):
    tk, xk = tiles[k], xt[k % 2]
    nk = tk["n"]
    K.S.dma("sync", lambda e: e.dma_start(out=xk.t[0:nk, :], in_=dap(xh[tk["xi"]], tk["row0"] * D, D, nk, D)),
            writes=[xk.b], dsem=xk.ds)


def phase_proj0(K):
    nc, S, X, O, SC = K.nc, K.S, K.X, K.O, K.SC
    with ExitStack() as ph:
        mk = lambda name, shape, dt, psum=False: T(K, ph, name, shape, dt, psum)
        K.ropeb = Buf("ropetmp")
        w_in = mk("w_in", [128, 8, A_IN], BF16)
        w_uq = mk("w_uq", [128, 4, 768], BF16)
        w_ukv = mk("w_ukv", [128, 2, 1024], BF16)
        gmix, gq, gkv, bfb = mk("gmix", [128, D], F32), mk("gq", [128, 512], F32), mk("gkv", [128, 256], F32), \
            mk("bfb", [128, 8], F32)
        stg = [mk(f"wst{i}", [128, 1024], F32) for i in range(2)]
        Bw = [Buf("w0"), Buf("w1")]
        load_w(K, X.a_w_in, 0, A_IN, 8, A_IN, w_in, stg, Bw)
        load_w(K, X.a_w_uq, 0, 768, 4, 768, w_uq, stg, Bw)
        load_w(K, X.a_w_ukv, 0, 1024, 2, 1024, w_ukv, stg, Bw)
        load_bc(K, X.g_mix, 0, D, gmix)
        load_bc(K, X.a_g_q, 0, 512, gq)
        load_bc(K, X.a_g_kv, 0, 256, gkv)
        load_bc(K, X.a_b_f, 0, 8, bfb)
        WB = Bw + [gmix.b, gq.b, gkv.b, bfb.b]

        xt = [mk(f"xt{i}", [128, D], F32) for i in range(2)]
        junk = mk("junk", [128, D], BF16)
        ss, rs = mk("ss", [128, 1], F32), mk("rs", [128, 1], F32)
        xn = mk("xn", [128, D], BF16)
        hT = mk("hT", [128, 8, 128], BF16)
        qn = mk("qn", [128, 512], BF16)
        qnT = mk("qnT", [128, 4, 128], BF16)
        ckv_f, ckv_b = mk("ckv_f", [128, 256], F32), mk("ckv_b", [128, 256], BF16)
        ckvT = mk("ckvT", [128, 2, 128], BF16)
        kr_f, kr_b = mk("kr_f", [128, 32], F32), mk("kr_b", [128, 32], BF16)
        t1, t2 = mk("t1", [128, 8, 16], F32), mk("t2", [128, 8, 16], F32)
        q_f = mk("q_f", [128, 8, 96], F32)
        qm_b, km_b = mk("qm_b", [128, 8, 96], BF16), mk("km_b", [128, 8, 96], BF16)
        vcat = mk("vcat", [128, 1024], BF16)
        fq_b, fk_b = mk("fq_b", [128, 512], BF16), mk("fk_b", [128, 512], BF16)
        fk_f, fv_f = mk("fk_f", [128, 512], F32), mk("fv_f", [128, 512], F32)
        z, logf, cum_t = mk("z", [128, 8], F32), mk("logf", [128, 8], F32), mk("cum_t", [128, 8], F32)
        carry = [mk(f"carry{s}", [128, 8], F32) for s in range(3)]
        qmT_s, kmT_s = mk("qmT_s", [96, 8, 128], BF16), mk("kmT_s", [96, 8, 128], BF16)
        qfT_s, kfT_s = mk("qfT_s", [128, 4, 128], BF16), mk("kfT_s", [128, 4, 128], BF16)
        cst = mk("cst", [128, 1024], F32)
        lsp, lres = mk("lsp", [128, 24], BF16), mk("lres", [128, 8], F32)
        pb = [mk(f"pb{i}", [128, 512], F32, True) for i in range(6)]
        pt = [mk(f"pt{i}", [128, 1024], BF16, True) for i in range(2)]
        for s in range(3):
            S.op("gpsimd", lambda e, s=s: e.memset(carry[s].t[:], 0.0), writes=[carry[s].b])

        def kvside(n, seq, kpos0, lf_t):
            sc = SC[seq]
            LP = K.SEQS[seq]["LP"]
            transpose_blocks(K, ckv_b, n, [(0, 128), (128, 128)], pt[1], ckvT, "scalar")
            if KV_STEP < 1:
                return
            for half in range(2):
                for c in range(2):
                    S.op("tensor", lambda e, half=half, c=c: e.matmul(
                        pb[1 + half].t[0:n, :], lhsT=ckvT.t[:, c, 0:n], rhs=w_ukv.t[:, c, half * 512:(half + 1) * 512],
                        start=(c == 0), stop=(c == 1)), reads=[ckvT.b] + WB, writes=[pb[1 + half].b])
            for half in range(2):
                if KV_SUB < 1:
                    break
                pv = pb[1 + half].t[:].rearrange("p (h c) -> p h c", c=128)
                S.op("vector", lambda e, half=half, pv=pv: e.tensor_copy(
                    out=km_b.t[0:n, half * 4:(half + 1) * 4, 0:64], in_=pv[0:n, :, 0:64]),
                    reads=[pb[1 + half].b], writes=[km_b.b])
                if KV_SUB < 2:
                    continue
                vv = vcat.t[:, 0:512].rearrange("p (h c) -> p h c", c=64)
                S.op("vector", lambda e, half=half, pv=pv, vv=vv: e.tensor_copy(
                    out=vv[0:n, half * 4:(half + 1) * 4, :], in_=pv[0:n, :, 64:128]),
                    reads=[pb[1 + half].b], writes=[vcat.b])
            if KV_STEP < 2:
                return
            for hh in range(8):
                S.op("gpsimd", lambda e, hh=hh: e.tensor_copy(out=km_b.t[0:n, hh, 64:96], in_=kr_b.t[0:n, :]),
                     reads=[kr_b.b], writes=[km_b.b])
            if KV_STEP < 3:
                return
            transpose_blocks(K, km_b, n, [(h * 96, 96) for h in range(8)], pt[0], kmT_s, "vector")
            if KV_STEP < 4:
                return
            store(K, bass.AP(tensor=sc.kmT, offset=kpos0, ap=[[LP, 96], [96 * LP, 8], [1, n]]),
                  kmT_s.t[0:96, :, 0:n], kmT_s)
            if KV_STEP < 5:
                return
            transpose_blocks(K, fk_b, n, [(b * 128, 128) for b in range(4)], pt[1], kfT_s, "scalar")
            store(K, bass.AP(tensor=sc.kfT, offset=kpos0, ap=[[LP, 128], [128 * LP, 4], [1, n]]),
                  kfT_s.t[:, :, 0:n], kfT_s)
            if KV_STEP < 6:
                return
            store(K, dap(sc.v0, kpos0 * 1024, 1024, n, 1024), vcat.t[0:n, :], vcat)
            if KV_STEP < 7:
                return
            cr = carry[seq]
            S.op("vector", lambda e: e.tensor_copy(out=lsp.t[0:n, 0:8], in_=lf_t.t[0:n, 0:8]), reads=[lf_t.b],
                 writes=[lsp.b])
            S.op("vector", lambda e: e.tensor_tensor(out=lres.t[0:n, :], in0=lf_t.t[0:n, 0:8], in1=lsp.t[0:n, 0:8],
                                                     op=ALU.subtract), reads=[lf_t.b, lsp.b], writes=[lres.b])
            S.op("vector", lambda e: e.tensor_copy(out=lsp.t[0:n, 8:16], in_=lres.t[0:n, :]), reads=[lres.b],
                 writes=[lsp.b])
            S.op("vector", lambda e: e.tensor_tensor(out=lres.t[0:n, :], in0=lres.t[0:n, :], in1=lsp.t[0:n, 8:16],
                                                     op=ALU.subtract), reads=[lres.b, lsp.b], writes=[lres.b])
            S.op("vector", lambda e: e.tensor_copy(out=lsp.t[0:n, 16:24], in_=lres.t[0:n, :]), reads=[lres.b],
                 writes=[lsp.b])
            for k3 in range(3):
                S.op("tensor", lambda e, k3=k3: e.matmul(pb[3].t[0:n, 0:8], lhsT=K.tri_b[0:n, 0:n],
                                                         rhs=lsp.t[0:n, k3 * 8:(k3 + 1) * 8],
                                                         start=(k3 == 0), stop=(k3 == 2)),
                     reads=[lsp.b], writes=[pb[3].b])
            for k3 in range(3):
                S.op("tensor", lambda e, k3=k3: e.matmul(pb[3].t[:, 8:16], lhsT=K.ones_b[0:n, :],
                                                         rhs=lsp.t[0:n, k3 * 8:(k3 + 1) * 8],
                                                         start=(k3 == 0), stop=(k3 == 2)),
                     reads=[lsp.b], writes=[pb[3].b])
            S.op("vector", lambda e: e.tensor_tensor(out=cum_t.t[0:n, :], in0=pb[3].t[0:n, 0:8], in1=cr.t[0:n, :],
                                                     op=ALU.add), reads=[pb[3].b, cr.b], writes=[cum_t.b])
            S.op("vector", lambda e: e.tensor_tensor(out=cr.t[:], in0=pb[3].t[:, 8:16], in1=cr.t[:], op=ALU.add),
                 reads=[pb[3].b, cr.b], writes=[cr.b])
            if KV_STEP < 8:
                return
            store(K, dap(sc.cum, kpos0 * 8, 8, n, 8), cum_t.t[0:n, :], cum_t)

        for s in range(NS):
            for i in range(PAST // 128 if DBG_CACHE is None else DBG_CACHE):
                r0 = (s * PAST + i * 128)
                S.dma("sync", lambda e, r0=r0: e.dma_start(out=cst.t[:, 0:256], in_=dap(X.c_ckv, r0 * 256, 256, 128, 256)),
                      writes=[cst.b], dsem=cst.ds)
                S.op("vector", lambda e: e.tensor_copy(out=ckv_b.t[:], in_=cst.t[:, 0:256]), reads=[cst.b],
                     writes=[ckv_b.b])
                S.dma("sync", lambda e, r0=r0: e.dma_start(out=kr_f.t[:], in_=dap(X.c_krope, r0 * 32, 32, 128, 32)),
                      writes=[kr_f.b], dsem=kr_f.ds)
                S.op("vector", lambda e: e.tensor_copy(out=kr_b.t[:], in_=kr_f.t[:]), reads=[kr_f.b], writes=[kr_b.b])
                S.dma("sync", lambda e, r0=r0: e.dma_start(out=fk_f.t[:], in_=dap(X.c_fk, r0 * 512, 512, 128, 512)),
                      writes=[fk_f.b], dsem=fk_f.ds)
                S.op("gpsimd", lambda e: e.tensor_copy(out=fk_b.t[:], in_=fk_f.t[:]), reads=[fk_f.b], writes=[fk_b.b])
                S.dma("sync", lambda e, r0=r0: e.dma_start(out=fv_f.t[:], in_=dap(X.c_fv, r0 * 512, 512, 128, 512)),
                      writes=[fv_f.b], dsem=fv_f.ds)
                S.op("gpsimd", lambda e: e.tensor_copy(out=vcat.t[:, 512:1024], in_=fv_f.t[:]), reads=[fv_f.b],
                     writes=[vcat.b])
                S.dma("sync", lambda e, r0=r0: e.dma_start(out=logf.t[:], in_=dap(X.c_logf, r0 * 8, 8, 128, 8)),
                      writes=[logf.b], dsem=logf.ds)
                kvside(128, 1 + s, i * 128, logf)

        tiles = token_tiles(K)
        xh = [X.xp, X.xs]
        for it, tl in enumerate(tiles):
            n, seq, pos0, row0, oi = tl["n"], tl["seq"], tl["pos0"], tl["row0"], tl["oi"]
            xb = xt[it % 2]
            if it == 0:
                _ld_x(K, tiles, xt, xh, 0)
            if it + 1 < len(tiles):
                _ld_x(K, tiles, xt, xh, it + 1)
            if DBG_STEP < 1:
                continue
            rms_stats(K, xb.t[0:n, :], n, D, 1e-6, junk, ss, rs, [xb.b])
            S.op("vector", lambda e, xb=xb, n=n: e.scalar_tensor_tensor(out=xn.t[0:n, :], in0=xb.t[0:n, :],
                                                                        scalar=rs.t[0:n, 0:1], in1=gmix.t[0:n, :],
                                                                        op0=ALU.mult, op1=ALU.mult),
                 reads=[xb.b, rs.b, gmix.b], writes=[xn.b])
            if DBG_STEP < 2:
                continue
            transpose_blocks(K, xn, n, [(c * 128, 128) for c in range(8)], pt[0], hT, "scalar")
            groups = [(pb[0], 0, 0, 512), (pb[1], 0, 512, 288), (pb[1], 288, 2336, 8), (pb[2], 0, 800, 512),
                      (pb[3], 0, 1312, 512), (pb[4], 0, 1824, 512)]
            for (pt_, pc0, wc0, wd) in groups:
                for c in range(8):
                    S.op("tensor", lambda e, pt_=pt_, pc0=pc0, wc0=wc0, wd=wd, c=c, n=n: e.matmul(
                        pt_.t[0:n, pc0:pc0 + wd], lhsT=hT.t[:, c, 0:n], rhs=w_in.t[:, c, wc0:wc0 + wd],
                        start=(c == 0), stop=(c == 7)), reads=[hT.b] + WB, writes=[pt_.b])
            if DBG_STEP == 3 and it == 0:
                dg = [mk(f"dg{i}", [128, 1024], F32) for i in range(4)]
                S.op("vector", lambda e: e.tensor_copy(out=dg[0].t[:], in_=xn.t[:]), reads=[xn.b], writes=[dg[0].b])
                S.op("vector", lambda e: e.tensor_copy(out=dg[1].t[:], in_=hT.t[:].rearrange("p a b -> p (a b)")), reads=[hT.b], writes=[dg[1].b])
                S.op("vector", lambda e: e.tensor_copy(out=dg[2].t[:, 0:512], in_=pb[0].t[:]), reads=[pb[0].b], writes=[dg[2].b])
                S.op("vector", lambda e: e.tensor_copy(out=dg[2].t[:, 512:1024], in_=pb[1].t[:]), reads=[pb[1].b], writes=[dg[2].b])
                S.op("vector", lambda e: e.tensor_copy(out=dg[3].t[:, 0:128], in_=K.ident[:]), reads=[], writes=[dg[3].b])
                S.op("vector", lambda e: e.tensor_copy(out=dg[3].t[:, 128:129], in_=rs.t[:]), reads=[rs.b], writes=[dg[3].b])
                S.op("vector", lambda e: e.tensor_copy(out=dg[3].t[:, 256:272], in_=K.cosT[:, 0, :]), reads=[], writes=[dg[3].b])
                S.op("vector", lambda e: e.tensor_copy(out=dg[3].t[:, 272:288], in_=K.sinT[:, 0, :]), reads=[], writes=[dg[3].b])
                S.op("vector", lambda e: e.tensor_copy(out=dg[3].t[:, 288:304], in_=K.cosT[:, 63, :]), reads=[], writes=[dg[3].b])
                for i in range(4):
                    store(K, dap(O.dbg[i], 0, 1024, 128, 1024), dg[i].t[:], dg[i])
            if DBG_STEP < 4:
                continue
            rms_stats(K, pb[0].t[0:n, :], n, 512, 1e-6, junk, ss, rs, [pb[0].b])
            S.op("vector", lambda e, n=n: e.scalar_tensor_tensor(out=qn.t[0:n, :], in0=pb[0].t[0:n, :],
                                                                 scalar=rs.t[0:n, 0:1], in1=gq.t[0:n, :],
                                                                 op0=ALU.mult, op1=ALU.mult),
                 reads=[pb[0].b, rs.b, gq.b], writes=[qn.b])
            rms_stats(K, pb[1].t[0:n, 0:256], n, 256, 1e-6, junk, ss, rs, [pb[1].b])
            S.op("vector", lambda e, n=n: e.scalar_tensor_tensor(out=ckv_f.t[0:n, :], in0=pb[1].t[0:n, 0:256],
                                                                 scalar=rs.t[0:n, 0:1], in1=gkv.t[0:n, :],
                                                                 op0=ALU.mult, op1=ALU.mult),
                 reads=[pb[1].b, rs.b, gkv.b], writes=[ckv_f.b])
            S.op("gpsimd", lambda e, n=n: e.tensor_copy(out=ckv_b.t[0:n, :], in_=ckv_f.t[0:n, :]), reads=[ckv_f.b],
                 writes=[ckv_b.b])
            store(K, dap(O.ckv[oi], row0 * 256, 256, n, 256), ckv_f.t[0:n, :], ckv_f)
            if DBG_STEP < 4:
                continue
            cos, sin = rope_tables(K, tl, n)
            rope(K, pb[1].t[0:n, 256:272], pb[1].t[0:n, 272:288], kr_f.t[0:n, 0:16], kr_f.t[0:n, 16:32], cos, sin,
                 t1.t[0:n, 0, :], t2.t[0:n, 0, :], [pb[1].b], [kr_f.b])
            S.op("gpsimd", lambda e, n=n: e.tensor_copy(out=kr_b.t[0:n, :], in_=kr_f.t[0:n, :]), reads=[kr_f.b],
                 writes=[kr_b.b])
            store(K, dap(O.krope[oi], row0 * 32, 32, n, 32), kr_f.t[0:n, :], kr_f)
            if DBG_STEP < 5:
                continue
            S.op("vector", lambda e, n=n: e.tensor_tensor(out=z.t[0:n, :], in0=pb[1].t[0:n, 288:296], in1=bfb.t[0:n, :],
                                                          op=ALU.add), reads=[pb[1].b, bfb.b], writes=[z.b])
            S.op("scalar", lambda e, n=n: e.activation(out=z.t[0:n, :], in_=z.t[0:n, :], func=AF.Exp, scale=-1.0),
                 reads=[z.b], writes=[z.b])
            S.op("scalar", lambda e, n=n: e.activation(out=z.t[0:n, :], in_=z.t[0:n, :], func=AF.Ln,
                                                       bias=K.onec[0:n, :]), reads=[z.b], writes=[z.b])
            S.op("vector", lambda e, n=n: e.tensor_scalar(out=logf.t[0:n, :], in0=z.t[0:n, :], scalar1=-1.0,
                                                          scalar2=None, op0=ALU.mult), reads=[z.b], writes=[logf.b])
            store(K, dap(O.logf[oi], row0 * 8, 8, n, 8), logf.t[0:n, :], logf)
            if DBG_STEP < 6:
                continue
            S.op("scalar", lambda e, n=n: e.activation(out=fq_b.t[0:n, :], in_=pb[2].t[0:n, :], func=AF.Copy),
                 reads=[pb[2].b], writes=[fq_b.b])
            S.op("vector", lambda e, n=n: e.tensor_copy(out=fk_f.t[0:n, :], in_=pb[3].t[0:n, :]), reads=[pb[3].b],
                 writes=[fk_f.b])
            S.op("gpsimd", lambda e, n=n: e.tensor_copy(out=fk_b.t[0:n, :], in_=fk_f.t[0:n, :]), reads=[fk_f.b],
                 writes=[fk_b.b])
            store(K, dap(O.fk[oi], row0 * 512, 512, n, 512), fk_f.t[0:n, :], fk_f)
            S.op("scalar", lambda e, n=n: e.activation(out=fv_f.t[0:n, :], in_=pb[4].t[0:n, :], func=AF.Copy),
                 reads=[pb[4].b], writes=[fv_f.b])
            S.op("gpsimd", lambda e, n=n: e.tensor_copy(out=vcat.t[0:n, 512:1024], in_=fv_f.t[0:n, :]),
                 reads=[fv_f.b], writes=[vcat.b])
            store(K, dap(O.fv[oi], row0 * 512, 512, n, 512), fv_f.t[0:n, :], fv_f)
            if DBG_STEP < 7:
                continue
            transpose_blocks(K, qn, n, [(c * 128, 128) for c in range(4)], pt[1], qnT, "scalar")
            for (pt_, wc0, wd) in [(pb[5], 0, 512), (pb[0], 512, 256)]:
                for c in range(4):
                    S.op("tensor", lambda e, pt_=pt_, wc0=wc0, wd=wd, c=c, n=n: e.matmul(
                        pt_.t[0:n, 0:wd], lhsT=qnT.t[:, c, 0:n], rhs=w_uq.t[:, c, wc0:wc0 + wd],
                        start=(c == 0), stop=(c == 3)), reads=[qnT.b] + WB, writes=[pt_.b])
            qf2 = q_f.t[:].rearrange("p h c -> p (h c)")
            S.op("vector", lambda e, n=n: e.tensor_copy(out=qf2[0:n, 0:512], in_=pb[5].t[0:n, :]), reads=[pb[5].b],
                 writes=[q_f.b])
            S.op("scalar", lambda e, n=n: e.activation(out=qf2[0:n, 512:768], in_=pb[0].t[0:n, 0:256], func=AF.Copy),
                 reads=[pb[0].b], writes=[q_f.b])
            S.op("gpsimd", lambda e, n=n: e.tensor_copy(out=qm_b.t[0:n, :, 0:64], in_=q_f.t[0:n, :, 0:64]),
                 reads=[q_f.b], writes=[qm_b.b])
            rope(K, q_f.t[0:n, :, 64:80], q_f.t[0:n, :, 80:96], qm_b.t[0:n, :, 64:80], qm_b.t[0:n, :, 80:96],
                 bcast_mid(cos, 8), bcast_mid(sin, 8), t1.t[0:n, :, :], t2.t[0:n, :, :], [q_f.b], [qm_b.b])
            sc = SC[seq]
            TqP = max(K.SEQS[seq]["Tq"], 32)
            qcol0 = pos0 - K.SEQS[seq]["past"]
            transpose_blocks(K, qm_b, n, [(h * 96, 96) for h in range(8)], pt[0], qmT_s, "vector")
            store(K, bass.AP(tensor=sc.qmT, offset=qcol0, ap=[[TqP, 96], [96 * TqP, 8], [1, n]]),
                  qmT_s.t[0:96, :, 0:n], qmT_s)
            transpose_blocks(K, fq_b, n, [(b * 128, 128) for b in range(4)], pt[1], qfT_s, "scalar")
            store(K, bass.AP(tensor=sc.qfT, offset=qcol0, ap=[[TqP, 128], [128 * TqP, 4], [1, n]]),
                  qfT_s.t[:, :, 0:n], qfT_s)
            if DBG_STEP < 8:
                continue
            kvside(n, seq, pos0, logf)
        S.barrier()
        S.emit()


_NC_CACHE = {}


def kernel(**inp):
    f = lambda a: np.ascontiguousarray(np.asarray(a, dtype=np.float32))
    if "nc" not in _NC_CACHE:
        _NC_CACHE["nc"] = build_program()
    nc = _NC_CACHE["nc"]
    shared = {
        "g_mix": f(inp["g_mix"]), "a_w_in": f(inp["a_w_in"][0]), "a_g_q": f(inp["a_g_q"][0]).reshape(1, 512),
        "a_w_uq": f(inp["a_w_uq"][0]), "a_g_kv": f(inp["a_g_kv"][0]).reshape(1, 256),
        "a_w_ukv": f(inp["a_w_ukv"][0]), "a_b_f": f(inp["a_b_f"][0]).reshape(1, 8),
        "a_w_out": f(inp["a_w_out"][0]), "c_w_in": f(inp["c_w_in"][0]), "c_lam": f(inp["c_lam"][0]).reshape(1, 256),
        "c_g_sub": f(inp["c_g_sub"][0]).reshape(128, 1), "c_w_out": f(inp["c_w_out"][0]),
        "g_mlp": f(inp["g_mlp"]), "w_up": f(inp["w_up"]), "w_down": f(inp["w_down"]),
        "g_final": f(inp["g_final"]).reshape(1, D),
    }
    in_maps = []
    for c in range(NCORES):
        sl = slice(NS * c, NS * c + NS)
        m = dict(shared)
        m["xp"] = f(inp["x_prompt"][c])
        m["xs"] = f(inp["x_sample"][sl]).reshape(NS * DEC_SEQ, D)
        m["c_ckv"] = f(inp["cache_mla_ckv"][0, sl])
        m["c_krope"] = f(inp["cache_mla_krope"][0, sl])
        m["c_fk"] = f(inp["cache_fox_k"][0, sl]).reshape(NS, PAST, 512)
        m["c_fv"] = f(inp["cache_fox_v"][0, sl]).reshape(NS, PAST, 512)
        m["c_logf"] = f(inp["cache_fox_logf"][0, sl])
        m["c_dk"] = f(inp["cache_diff_k"][0, sl]).reshape(NS, PAST, 1024)
        m["c_dv"] = f(inp["cache_diff_v"][0, sl]).reshape(NS, PAST, 1024)
        in_maps.append(m)
    ncr = DBG_CORES
    res = run_bass_kernel_spmd(nc, in_maps[:ncr], core_ids=list(range(ncr)))
    R = list(res.results) + [res.results[0]] * (NCORES - ncr)

    def gp(name, shape):
        return np.stack([np.asarray(R[c][name], dtype=np.float32) for c in range(NCORES)]).reshape(shape)

    def gs(name, shape):
        return np.concatenate([np.asarray(R[c][name], dtype=np.float32).reshape(NS, DEC_SEQ, -1)
                               for c in range(NCORES)]).reshape(shape)

    global DBG_OUT
    DBG_OUT = [np.asarray(R[0][f"dbg{i}"]) for i in range(4)] if DBG_STEP == 3 else []
    B, SB = NCORES, NCORES * NS
    return (
        gp("y_p", (B, SEQ, D)), gs("y_s", (SB, DEC_SEQ, D)),
        gp("o_ckv_p", (1, B, SEQ, 256)), gp("o_krope_p", (1, B, SEQ, 32)),
        gp("o_fk_p", (1, B, SEQ, 8, 64)), gp("o_fv_p", (1, B, SEQ, 8, 64)), gp("o_logf_p", (1, B, SEQ, 8)),
        gp("o_dk_p", (1, B, SEQ, 8, 128)), gp("o_dv_p", (1, B, SEQ, 8, 128)),
        gs("o_ckv_s", (1, SB, DEC_SEQ, 256)), gs("o_krope_s", (1, SB, DEC_SEQ, 32)),
        gs("o_fk_s", (1, SB, DEC_SEQ, 8, 64)), gs("o_fv_s", (1, SB, DEC_SEQ, 8, 64)),
        gs("o_logf_s", (1, SB, DEC_SEQ, 8)),
        gs("o_dk_s", (1, SB, DEC_SEQ, 8, 128)), gs("o_dv_s", (1, SB, DEC_SEQ, 8, 128)),
    )


LAM_INIT = 0.8 - 0.6 * math.exp(-0.3 * 1)


def phase_attn(K, layer):
    nc, S, X, SC = K.nc, K.S, K.X, K.SC
    with ExitStack() as ph:
        mk = lambda name, shape, dt, psum=False: T(K, ph, name, shape, dt, psum)
        qT = [mk(f"aqT{i}", [128, SEQ], BF16) for i in range(2)]
        qT2 = [mk(f"aqTb{i}", [128, SEQ], BF16) for i in range(2)]
        kT = [mk(f"akT{i}", [128, SEQ], BF16) for i in range(2)]
        Vt = [mk(f"aV{i}", [128, 64, 192], BF16) for i in range(2)]
        rl2 = mk("arl2", [128, 512], F32)
        pT = [mk(f"apT{i}", [128, 512], BF16) for i in range(6)]
        rl = mk("arl", [128, 512], F32)
        ona, onb, comb, rr = (mk("aona", [128, 512], F32), mk("aonb", [128, 512], F32), mk("acomb", [128, 512], F32),
                              mk("arr", [128, 512], F32))
        sq = mk("asq", [128, 512], BF16)
        ob = [mk(f"aob{i}", [128, 512], BF16) for i in range(2)]
        cum_all = mk("acum", [128, 64, 8], F32)
        cs_all = mk("acs", [128, 64, 8], F32)
        biasq = [mk(f"abq{i}", [128, 4, 64], F32) for i in range(2)]
        NSP = 6 if layer == 0 else 3
        sp = [mk(f"asp{i}", [128, 512], F32, True) for i in range(NSP)]
        po = [mk(f"apo{i}", [128, 512], F32, True) for i in range(2)]
        pl = [mk(f"apl{i}", [128, 512], F32, True) for i in range(2)] if layer == 1 else [None, None]
        pm = mk("apm", [128, 512], F32, True) if layer == 1 else None
        neglam, gs = mk("aneglam", [128, 1], F32), mk("ags", [128, 1], F32)
        S.op("gpsimd", lambda e: e.memset(cum_all.t[:], 0.0), writes=[cum_all.b])
        S.op("gpsimd", lambda e: e.memset(cs_all.t[:], 0.0), writes=[cs_all.b])
        if layer == 0:
            for vt_ in Vt:
                S.op("gpsimd", lambda e, vt_=vt_: e.memset(vt_.t[:, :, 64:128], 1.0), writes=[vt_.b])
        cnt = Ctx()
        cnt.task = 0
        cnt.grp = 0
        cnt.ob = 0

        if layer == 1:
            lam_t, pr, sm = mk("alam", [1, 256], F32), mk("apr", [1, 128], F32), mk("asm", [1, 2], F32)
            lscr = nc.dram_tensor(f"lam_scr{layer}", [1, 1], F32, kind="Internal")
            S.dma("sync", lambda e: e.dma_start(out=lam_t.t[:], in_=dap(X.c_lam, 0, 256, 1, 256)), writes=[lam_t.b],
                  dsem=lam_t.ds)
            S.op("vector", lambda e: e.tensor_tensor(out=pr.t[:, 0:64], in0=lam_t.t[:, 0:64], in1=lam_t.t[:, 64:128],
                                                     op=ALU.mult), reads=[lam_t.b], writes=[pr.b])
            S.op("vector", lambda e: e.tensor_tensor(out=pr.t[:, 64:128], in0=lam_t.t[:, 128:192],
                                                     in1=lam_t.t[:, 192:256], op=ALU.mult), reads=[lam_t.b],
                 writes=[pr.b])
            S.op("vector", lambda e: e.tensor_reduce(out=sm.t[:, 0:2], in_=pr.t[:].rearrange("p (a b) -> p a b", b=64),
                                                     axis=mybir.AxisListType.X, op=ALU.add), reads=[pr.b],
                 writes=[sm.b])
            S.op("scalar", lambda e: e.activation(out=sm.t[:], in_=sm.t[:], func=AF.Exp), reads=[sm.b], writes=[sm.b])
            S.op("vector", lambda e: e.tensor_tensor(out=pr.t[:, 0:1], in0=sm.t[:, 1:2], in1=sm.t[:, 0:1],
                                                     op=ALU.subtract), reads=[sm.b], writes=[pr.b])
            S.op("vector", lambda e: e.tensor_scalar(out=pr.t[:, 0:1], in0=pr.t[:, 0:1], scalar1=-LAM_INIT,
                                                     scalar2=None, op0=ALU.add), reads=[pr.b], writes=[pr.b])
            Bl = Buf("lamscr")
            S.dma("sync", lambda e: e.dma_start(out=lscr.ap(), in_=pr.t[:, 0:1]), reads=[pr.b], writes=[Bl],
                  dsem=pr.ds)
            S.dma("sync", lambda e: e.dma_start(out=neglam.t[:], in_=bc_rows(lscr, 0, 1)), reads=[Bl],
                  writes=[neglam.b], dsem=neglam.ds)
            S.dma("sync", lambda e: e.dma_start(out=gs.t[:], in_=dap(X.c_g_sub, 0, 1, 128, 1)), writes=[gs.b],
                  dsem=gs.ds)
            S.op("vector", lambda e: e.tensor_scalar(out=gs.t[:], in0=gs.t[:], scalar1=1.0 - LAM_INIT, scalar2=None,
                                                     op0=ALU.mult), reads=[gs.b], writes=[gs.b])

        groups = []
        for s, sq_ in enumerate(K.SEQS):
            sc = SC[s]
            if layer == 0:
                for h in range(8):
                    groups.append(dict(seq=s, qsrc=(sc.qmT, h, 96), ksrc=(sc.kmT, h, 96),
                                       vsrc=[(sc.v0, h * 64, 64, 0)],
                                       maps=[dict(pb=0, dq=96, voff=0, e=64, scale=96 ** -0.5, mode="chunk",
                                                  fox=None, row0=h * 64, aug=(0, 0, 64))], diff=False))
                for p in range(4):
                    groups.append(dict(seq=s, qsrc=(sc.qfT, p, 128), ksrc=(sc.kfT, p, 128),
                                       vsrc=[(sc.v0, 512 + p * 128, 64, 0), (sc.v0, 512 + p * 128 + 64, 64, 128)],
                                       maps=[dict(pb=64 * m, dq=64, voff=64 * m, e=64, scale=0.125, mode="causal",
                                                  fox=2 * p + m, row0=512 + (2 * p + m) * 64,
                                                  aug=((0, 0, 64), (64, 64, 0))[m]) for m in range(2)],
                                       diff=False))
            else:
                for h in range(8):
                    groups.append(dict(seq=s, qsrc=(sc.qdT, h, 128), ksrc=(sc.kdT, h, 128),
                                       vsrc=[(sc.v1, h * 128, 128, 0)],
                                       maps=[dict(pb=64 * m, dq=64, voff=0, e=128, scale=0.125, mode="chunk",
                                                  fox=None, row0=h * 128, aug=None) for m in range(2)], diff=True))

        def load_group(g, slot):
            s = g["seq"]
            info = K.SEQS[s]
            Tq, L, LP = info["Tq"], info["L"], info["LP"]
            TqP = max(Tq, 32)
            qh, qi, qr = g["qsrc"]
            kh, ki_, kr = g["ksrc"]
            qt, kt, vt = qT[slot], kT[slot], Vt[slot]
            S.dma("sync", lambda e: e.dma_start(out=qt.t[0:qr, 0:Tq], in_=dap(qh, qi * qr * TqP, TqP, qr, Tq)),
                  writes=[qt.b], dsem=qt.ds)
            S.dma("sync", lambda e: e.dma_start(out=kt.t[0:kr, 0:L], in_=dap(kh, ki_ * kr * LP, LP, kr, L)),
                  writes=[kt.b], dsem=kt.ds)
            if len(g["maps"]) == 2:
                qt2 = qT2[slot]
                S.dma("sync", lambda e: e.dma_start(out=qt2.t[0:qr, 0:Tq], in_=dap(qh, qi * qr * TqP, TqP, qr, Tq)),
                      writes=[qt2.b], dsem=qt2.ds)
                S.op("gpsimd", lambda e: e.memset(qt.t[64:128, 0:Tq], 0.0), reads=[qt.b], writes=[qt.b])
                S.op("gpsimd", lambda e: e.memset(qt2.t[0:64, 0:Tq], 0.0), reads=[qt2.b], writes=[qt2.b])
            nfull = L // 128
            rem = L - nfull * 128
            for (vh, vc0, ve, dc) in g["vsrc"]:
                S.dma("sync", lambda e, vh=vh, vc0=vc0, ve=ve, dc=dc: e.dma_start(
                    out=vt.t[:, 0:nfull, dc:dc + ve],
                    in_=bass.AP(tensor=vh, offset=vc0, ap=[[1024, 128], [128 * 1024, nfull], [1, ve]])),
                    writes=[vt.b], dsem=vt.ds)
                if rem:
                    S.dma("sync", lambda e, vh=vh, vc0=vc0, ve=ve, dc=dc: e.dma_start(
                        out=vt.t[0:rem, nfull, dc:dc + ve],
                        in_=bass.AP(tensor=vh, offset=nfull * 128 * 1024 + vc0, ap=[[1024, rem], [1, ve]])),
                        writes=[vt.b], dsem=vt.ds)

        def load_fox_tables(s):
            info = K.SEQS[s]
            L, past, Tq = info["L"], info["past"], info["Tq"]
            nfull = L // 128
            cum = SC[s].cum
            S.dma("sync", lambda e: e.dma_start(
                out=cum_all.t[:, 0:nfull, :], in_=bass.AP(tensor=cum, offset=0, ap=[[8, 128], [1024, nfull], [1, 8]])),
                writes=[cum_all.b], dsem=cum_all.ds)
            rem = L - nfull * 128
            if rem:
                S.dma("sync", lambda e: e.dma_start(
                    out=cum_all.t[0:rem, nfull, :],
                    in_=bass.AP(tensor=cum, offset=nfull * 1024, ap=[[8, rem], [1, 8]])),
                    writes=[cum_all.b], dsem=cum_all.ds)
            nqb = max(1, Tq // 128)
            if past == 0:
                S.op("gpsimd", lambda e: e.memset(cs_all.t[:, 0, :], 0.0), writes=[cs_all.b])
                S.dma("sync", lambda e: e.dma_start(
                    out=cs_all.t[:, 1:nqb, :],
                    in_=bass.AP(tensor=cum, offset=127 * 8, ap=[[0, 128], [1024, nqb - 1], [1, 8]])),
                    writes=[cs_all.b], dsem=cs_all.ds)
            else:
                S.dma("sync", lambda e: e.dma_start(
                    out=cs_all.t[:, 0, :], in_=bass.AP(tensor=cum, offset=(past - 1) * 8, ap=[[0, 128], [1, 8]])),
                    writes=[cs_all.b], dsem=cs_all.ds)

        def run_group(g, slot):
            s = g["seq"]
            info = K.SEQS[s]
            Tq, L, past = info["Tq"], info["L"], info["past"]
            TqP = max(Tq, 32)
            QW = min(512, Tq)
            nqt = Tq // QW
            qt, kt, vt = qT[slot], kT[slot], Vt[slot]
            tasks = []
            for Tt in range(nqt):
                q0 = Tt * QW
                nkt = (past + q0 + QW + 127) // 128
                for mi, m in enumerate(g["maps"]):
                    grp = cnt.grp
                    cnt.grp += 1
                    for j in range(nkt):
                        tasks.append(dict(T=Tt, q0=q0, mi=mi, m=m, j=j, nkt=nkt, grp=grp))

            def emit_qk(t, ti):
                m, j, q0 = t["m"], t["j"], t["q0"]
                kw = min(128, L - j * 128)
                c0 = max(0, j * 128 - (past + q0))
                t["kw"], t["c0"], t["sp"] = kw, c0, sp[ti % NSP]
                spt = t["sp"]
                if len(g["maps"]) == 2:
                    qsel = qt if t["mi"] == 0 else qT2[slot]
                    r0_, r1_ = 0, 128
                else:
                    qsel = qt
                    r0_, r1_ = m["pb"], m["pb"] + m["dq"]
                S.op("tensor", lambda e: e.matmul(spt.t[0:kw, c0:QW], lhsT=kt.t[r0_:r1_, j * 128:j * 128 + kw],
                                                  rhs=qsel.t[r0_:r1_, q0 + c0:q0 + QW], start=True, stop=True),
                     reads=[kt.b, qsel.b], writes=[spt.b])

            def emit_rest(t, ti):
                m, j, q0, nkt, grp = t["m"], t["j"], t["q0"], t["nkt"], t["grp"]
                kw, c0, spt = t["kw"], t["c0"], t["sp"]
                e_ = m["e"]
                ptt = pT[ti % len(pT)]
                pot, plt = po[grp % 2], pl[grp % 2]
                diag = j * 128 >= past + q0
                bw = min(128, QW - c0)
                if m["fox"] is not None:
                    hh = m["fox"]
                    bq = biasq[grp % 2]
                    nblk = 1
                    if j == 0:
                        for b in range(nblk):
                            qb = (q0 // 128) + (2 if QW == 512 else 0)
                            S.op("vector", lambda e, b=b, qb=qb: e.tensor_scalar(
                                out=bq.t[:, b, 0:nkt], in0=cum_all.t[:, 0:nkt, hh], scalar1=-1.0,
                                scalar2=cs_all.t[:, qb, hh:hh + 1], op0=ALU.mult, op1=ALU.add),
                                reads=[cum_all.b, cs_all.b], writes=[bq.b])
                    for b in range(1):
                        a0, a1 = c0, QW
                        S.op("scalar", lambda e, b=b, a0=a0, a1=a1: e.activation(
                            out=ptt.t[0:kw, a0:a1], in_=spt.t[0:kw, a0:a1], func=AF.Exp, scale=m["scale"],
                            bias=bq.t[0:kw, b, j:j + 1]), reads=[spt.b, bq.b], writes=[ptt.b])
                else:
                    S.op("scalar", lambda e: e.activation(out=ptt.t[0:kw, c0:QW], in_=spt.t[0:kw, c0:QW], func=AF.Exp,
                                                          scale=m["scale"]), reads=[spt.b], writes=[ptt.b])
                if diag:
                    if m["mode"] == "chunk":
                        if kw > 64:
                            S.op("gpsimd", lambda e: e.memset(ptt.t[64:kw, c0:c0 + min(64, bw)], 0.0),
                                 reads=[ptt.b], writes=[ptt.b])
                    else:
                        S.op("gpsimd", lambda e: e.affine_select(
                            out=ptt.t[0:kw, c0:c0 + bw], in_=ptt.t[0:kw, c0:c0 + bw], pattern=[[1, bw]],
                            compare_op=ALU.is_ge, fill=0.0, base=0, channel_multiplier=-1),
                            reads=[ptt.b], writes=[ptt.b])
                voff = m["voff"]
                oT_h = SC[s].oT
                if m["aug"] is not None:
                    vc, o0, l0 = m["aug"]
                    S.op("tensor", lambda e: e.matmul(pot.t[0:128, c0:QW], lhsT=vt.t[0:kw, j, vc:vc + 128],
                                                      rhs=ptt.t[0:kw, c0:QW], start=(j == 0), stop=(j == nkt - 1)),
                         reads=[vt.b, ptt.b], writes=[pot.b])
                    if j != nkt - 1:
                        return
                    S.op("vector", lambda e: e.reciprocal(out=rl.t[l0:l0 + 64, 0:QW], in_=pot.t[l0:l0 + 64, 0:QW]),
                         reads=[pot.b], writes=[rl.b])
                    S.dma("sync", lambda e: e.dma_start(out=rl2.t[o0:o0 + 64, 0:QW], in_=rl.t[l0:l0 + 64, 0:QW]),
                          reads=[rl.b], writes=[rl2.b], dsem=rl2.ds)
                    obt = ob[cnt.ob % 2]
                    cnt.ob += 1
                    S.op("vector", lambda e: e.tensor_tensor(out=obt.t[o0:o0 + 64, 0:QW], in0=pot.t[o0:o0 + 64, 0:QW],
                                                             in1=rl2.t[o0:o0 + 64, 0:QW], op=ALU.mult),
                         reads=[pot.b, rl2.b], writes=[obt.b])
                    store(K, dap(oT_h, m["row0"] * TqP + q0, TqP, 64, QW), obt.t[o0:o0 + 64, 0:QW], obt)
                    return
                S.op("tensor", lambda e: e.matmul(pot.t[0:e_, c0:QW], lhsT=vt.t[0:kw, j, voff:voff + e_],
                                                  rhs=ptt.t[0:kw, c0:QW], start=(j == 0), stop=(j == nkt - 1)),
                     reads=[vt.b, ptt.b], writes=[pot.b])
                S.op("tensor", lambda e: e.matmul(plt.t[0:e_, c0:QW], lhsT=K.ones_b[0:kw, 0:e_],
                                                  rhs=ptt.t[0:kw, c0:QW], start=(j == 0), stop=(j == nkt - 1)),
                     reads=[ptt.b], writes=[plt.b])
                if j != nkt - 1:
                    return
                S.op("vector", lambda e: e.reciprocal(out=rl.t[0:e_, 0:QW], in_=plt.t[0:e_, 0:QW]), reads=[plt.b],
                     writes=[rl.b])
                dst = ona if t["mi"] == 0 else onb
                S.op("vector", lambda e: e.tensor_tensor(out=dst.t[:, 0:QW], in0=pot.t[:, 0:QW], in1=rl.t[:, 0:QW],
                                                         op=ALU.mult), reads=[pot.b, rl.b], writes=[dst.b])
                if t["mi"] == 0:
                    return
                S.op("vector", lambda e: e.scalar_tensor_tensor(out=comb.t[:, 0:QW], in0=onb.t[:, 0:QW],
                                                                scalar=neglam.t[:, 0:1], in1=ona.t[:, 0:QW],
                                                                op0=ALU.mult, op1=ALU.add),
                     reads=[ona.b, onb.b, neglam.b], writes=[comb.b])
                S.op("scalar", lambda e: e.activation(out=sq.t[:, 0:QW], in_=comb.t[:, 0:QW], func=AF.Square),
                     reads=[comb.b], writes=[sq.b])
                S.op("tensor", lambda e: e.matmul(pm.t[:, 0:QW], lhsT=K.mean_b[:, :], rhs=sq.t[:, 0:QW], start=True,
                                                  stop=True), reads=[sq.b], writes=[pm.b])
                S.op("scalar", lambda e: e.activation(out=rr.t[:, 0:QW], in_=pm.t[:, 0:QW], func=AF.Ln,
                                                      bias=K.eps5[:, :]), reads=[pm.b], writes=[rr.b])
                S.op("scalar", lambda e: e.activation(out=rr.t[:, 0:QW], in_=rr.t[:, 0:QW], func=AF.Exp, scale=-0.5),
                     reads=[rr.b], writes=[rr.b])
                obt = ob[cnt.ob % 2]
                cnt.ob += 1
                S.op("vector", lambda e: e.scalar_tensor_tensor(out=obt.t[:, 0:QW], in0=comb.t[:, 0:QW],
                                                                scalar=gs.t[:, 0:1], in1=rr.t[:, 0:QW],
                                                                op0=ALU.mult, op1=ALU.mult),
                     reads=[comb.b, gs.b, rr.b], writes=[obt.b])
                store(K, dap(oT_h, m["row0"] * TqP + q0, TqP, 128, QW), obt.t[:, 0:QW], obt)

            LA = NSP - 1
            base = cnt.task
            for i in range(len(tasks) + LA):
                if i < len(tasks):
                    emit_qk(tasks[i], base + i)
                if i >= LA:
                    emit_rest(tasks[i - LA], base + i - LA)
            cnt.task += len(tasks)

        gl = groups if DBG_GROUPS is None else [groups[i] for i in DBG_GROUPS]
        cur_fox_seq = None
        if gl:
            load_group(gl[0], 0)
        for gi, g in enumerate(gl):
            if gi + 1 < len(gl):
                load_group(gl[gi + 1], (gi + 1) % 2)
            if g["maps"][0]["fox"] is not None and cur_fox_seq != g["seq"]:
                load_fox_tables(g["seq"])
                cur_fox_seq = g["seq"]
            run_group(g, gi % 2)
        S.barrier()
        S.emit()


def phase_mlp(K, layer):
    nc, S, X, O, SC = K.nc, K.S, K.X, K.O, K.SC
    w_o_h = X.a_w_out if layer == 0 else X.c_w_out
    with ExitStack() as ph:
        mk = lambda name, shape, dt, psum=False: T(K, ph, name, shape, dt, psum)
        w_out = mk("mw_out", [128, 8, D], BF16)
        w_up = mk("mw_up", [128, 8, D_FF], BF16)
        w_dn = mk("mw_dn", [128, 32, D], BF16)
        gm = mk("mgm", [128, D], F32)
        gf = mk("mgf", [128, D], F32) if layer == 1 else None
        stg = [mk(f"mst{i}", [128, 512], F32) for i in range(2)]
        Bw = [Buf("mw0"), Buf("mw1")]
        load_w(K, w_o_h, 0, D, 8, D, w_out, stg, Bw)
        load_w(K, X.w_up, layer * D * D_FF, D_FF, 8, D_FF, w_up, stg, Bw)
        load_w(K, X.w_down, layer * D_FF * D, D, 32, D, w_dn, stg, Bw)
        load_bc(K, X.g_mlp, layer * D, D, gm)
        WB = Bw + [gm.b]
        if layer == 1:
            load_bc(K, X.g_final, 0, D, gf)
            WB.append(gf.b)
        oTt = mk("moT", [128, 8, 256], BF16)
        xt = [mk(f"mx{i}", [128, D], F32) for i in range(2)]
        ss, rs = mk("mss", [128, 1], F32), mk("mrs", [128, 1], F32)
        xn = mk("mxn", [128, D], BF16)
        h2T = mk("mh2T", [128, 8, 256], BF16)
        uT = mk("muT", [128, 32, 256], BF16)
        rt = mk("mrt", [128, 256], F32)
        pmix = [mk(f"mpmix{i}", [128, 512], F32, True) for i in range(2)]
        ptr = mk("mptr", [128, 1024], BF16, True)
        pu = [mk(f"mpu{i}", [128, 512], F32, True) for i in range(2)]
        pd = [mk(f"mpd{i}", [128, 512], F32, True) for i in range(2)]

        PL = K.SEQS[0]["Tq"]
        macros = []
        xsrc_p = X.xp if layer == 0 else K.x1[0]
        xsrc_s = X.xs if layer == 0 else K.x1[1]
        for t0 in range(0, PL, 256):
            macros.append(dict(
                subs=[dict(n=128, xsrc=dap(xsrc_p, (t0 + i * 128) * D, D, 128, D),
                           dst=dap(K.x1[0] if layer == 0 else O.y[0], (t0 + i * 128) * D, D, 128, D)) for i in range(2)],
                oloads=[(SC[0].oT, max(PL, 32), t0, 256, 0)]))
        macros.append(dict(
            subs=[dict(n=NS * DEC_SEQ, xsrc=dap(xsrc_s, 0, D, NS * DEC_SEQ, D),
                       dst=dap(K.x1[1] if layer == 0 else O.y[1], 0, D, NS * DEC_SEQ, D))],
            oloads=[(SC[1 + s].oT, DEC_SEQ, 0, DEC_SEQ, s * DEC_SEQ) for s in range(NS)]))

        for mc in macros:
            for (oh, TqP, c0, ncols, d0) in mc["oloads"]:
                S.dma("sync", lambda e, oh=oh, TqP=TqP, c0=c0, ncols=ncols, d0=d0: e.dma_start(
                    out=oTt.t[:, :, d0:d0 + ncols],
                    in_=bass.AP(tensor=oh, offset=c0, ap=[[TqP, 128], [128 * TqP, 8], [1, ncols]])),
                    writes=[oTt.b], dsem=oTt.ds)
            ntok = sum(sbt["n"] for sbt in mc["subs"])
            for si, sbt in enumerate(mc["subs"]):
                n = sbt["n"]
                c_lo = si * 128
                x_ = xt[si]
                S.dma("sync", lambda e, x_=x_, sbt=sbt, n=n: e.dma_start(out=x_.t[0:n, :], in_=sbt["xsrc"]),
                      writes=[x_.b], dsem=x_.ds)
                for half in range(2):
                    for c in range(8):
                        S.op("tensor", lambda e, half=half, c=c, n=n, c_lo=c_lo: e.matmul(
                            pmix[half].t[0:n, :], lhsT=oTt.t[:, c, c_lo:c_lo + n],
                            rhs=w_out.t[:, c, half * 512:(half + 1) * 512], start=(c == 0), stop=(c == 7)),
                            reads=[oTt.b] + WB, writes=[pmix[half].b])
                for half in range(2):
                    S.op("vector", lambda e, half=half, n=n, x_=x_: e.tensor_tensor(
                        out=x_.t[0:n, half * 512:(half + 1) * 512], in0=x_.t[0:n, half * 512:(half + 1) * 512],
                        in1=pmix[half].t[0:n, :], op=ALU.add), reads=[x_.b, pmix[half].b], writes=[x_.b])
                rms_stats(K, x_.t[0:n, :], n, D, 1e-6, xn, ss, rs, [x_.b])
                S.op("vector", lambda e, n=n, x_=x_: e.scalar_tensor_tensor(
                    out=xn.t[0:n, :], in0=x_.t[0:n, :], scalar=rs.t[0:n, 0:1], in1=gm.t[0:n, :], op0=ALU.mult,
                    op1=ALU.mult), reads=[x_.b, rs.b, gm.b], writes=[xn.b])
                pv = ptr.t[:].rearrange("p (b c) -> p b c", c=128)
                for c in range(8):
                    S.op("tensor", lambda e, c=c, n=n: e.transpose(out=pv[:, c, 0:n], in_=xn.t[0:n, c * 128:(c + 1) * 128],
                                                                   identity=K.ident[0:n, 0:n]),
                         reads=[xn.b], writes=[ptr.b])
                S.op("scalar", lambda e, n=n, c_lo=c_lo: e.activation(out=h2T.t[:, :, c_lo:c_lo + n], in_=pv[:, :, 0:n],
                                                                      func=AF.Copy), reads=[ptr.b], writes=[h2T.b])
            for fc in range(32):
                put = pu[fc % 2]
                for c in range(8):
                    S.op("tensor", lambda e, fc=fc, c=c, put=put, ntok=ntok: e.matmul(
                        put.t[:, 0:ntok], lhsT=w_up.t[:, c, fc * 128:(fc + 1) * 128], rhs=h2T.t[:, c, 0:ntok],
                        start=(c == 0), stop=(c == 7)), reads=[h2T.b] + WB, writes=[put.b])
                S.op("scalar", lambda e, put=put, ntok=ntok: e.activation(out=rt.t[:, 0:ntok], in_=put.t[:, 0:ntok], func=AF.Relu),
                     reads=[put.b], writes=[rt.b])
                S.op("vector", lambda e, fc=fc, put=put, ntok=ntok: e.tensor_tensor(out=uT.t[:, fc, 0:ntok], in0=rt.t[:, 0:ntok],
                                                                         in1=put.t[:, 0:ntok], op=ALU.mult),
                     reads=[rt.b, put.b], writes=[uT.b])
            for si, sbt in enumerate(mc["subs"]):
                n = sbt["n"]
                c_lo = si * 128
                x_ = xt[si]
                for half in range(2):
                    for fc in range(32):
                        S.op("tensor", lambda e, half=half, fc=fc, n=n, c_lo=c_lo: e.matmul(
                            pd[half].t[0:n, :], lhsT=uT.t[:, fc, c_lo:c_lo + n],
                            rhs=w_dn.t[:, fc, half * 512:(half + 1) * 512], start=(fc == 0), stop=(fc == 31)),
                            reads=[uT.b] + WB, writes=[pd[half].b])
                for half in range(2):
                    S.op("vector", lambda e, half=half, n=n, x_=x_: e.tensor_tensor(
                        out=x_.t[0:n, half * 512:(half + 1) * 512], in0=x_.t[0:n, half * 512:(half + 1) * 512],
                        in1=pd[half].t[0:n, :], op=ALU.add), reads=[x_.b, pd[half].b], writes=[x_.b])
                if layer == 0:
                    store(K, sbt["dst"], x_.t[0:n, :], x_)
                else:
                    rms_stats(K, x_.t[0:n, :], n, D, 1e-6, xn, ss, rs, [x_.b])
                    S.op("vector", lambda e, n=n, x_=x_: e.scalar_tensor_tensor(
                        out=x_.t[0:n, :], in0=x_.t[0:n, :], scalar=rs.t[0:n, 0:1], in1=gf.t[0:n, :], op0=ALU.mult,
                        op1=ALU.mult), reads=[x_.b, rs.b, gf.b], writes=[x_.b])
                    store(K, sbt["dst"], x_.t[0:n, :], x_)
        S.barrier()
        S.emit()


def phase_proj1(K):
    nc, S, X, O, SC = K.nc, K.S, K.X, K.O, K.SC
    with ExitStack() as ph:
        mk = lambda name, shape, dt, psum=False: T(K, ph, name, shape, dt, psum)
        K.ropeb = Buf("ropetmp1")
        w_c = mk("cw_in", [128, 8, 3072], BF16)
        gmix = mk("cgmix", [128, D], F32)
        stg = [mk(f"cst{i}", [128, 1024], F32) for i in range(2)]
        Bw = [Buf("cw0"), Buf("cw1")]
        load_w(K, X.c_w_in, 0, 3072, 8, 3072, w_c, stg, Bw)
        load_bc(K, X.g_mix, D, D, gmix)
        WB = Bw + [gmix.b]
        xt = [mk(f"cx{i}", [128, D], F32) for i in range(2)]
        junk = mk("cjunk", [128, D], BF16)
        ss, rs = mk("css", [128, 1], F32), mk("crs", [128, 1], F32)
        xn = mk("cxn", [128, D], BF16)
        hT = mk("chT", [128, 8, 128], BF16)
        t1, t2 = mk("ct1", [128, 8, 8], F32), mk("ct2", [128, 8, 8], F32)
        qd_b, kd_b, v_b = mk("cqd_b", [128, 16, 64], BF16), mk("ckd_b", [128, 16, 64], BF16), mk("cv_b", [128, D], BF16)
        k_f, v_f = mk("ck_f", [128, 16, 64], F32), mk("cv_f", [128, D], F32)
        qdT_s, kdT_s = mk("cqdT_s", [128, 8, 128], BF16), mk("ckdT_s", [128, 8, 128], BF16)
        pb = [mk(f"cpb{i}", [128, 512], F32, True) for i in range(6)]
        pt = [mk(f"cpt{i}", [128, 1024], BF16, True) for i in range(2)]

        def kside(n, seq, kpos0):
            sc = SC[seq]
            LP = K.SEQS[seq]["LP"]
            transpose_blocks(K, kd_b, n, [(b * 128, 128) for b in range(8)], pt[1], kdT_s, "vector")
            store(K, bass.AP(tensor=sc.kdT, offset=kpos0, ap=[[LP, 128], [128 * LP, 8], [1, n]]),
                  kdT_s.t[:, :, 0:n], kdT_s)
            store(K, dap(sc.v1, kpos0 * 1024, 1024, n, 1024), v_b.t[0:n, :], v_b)

        kf2 = k_f.t[:].rearrange("p h c -> p (h c)")
        kb2 = kd_b.t[:].rearrange("p h c -> p (h c)")
        for s in range(NS):
            for i in range(PAST // 128 if DBG_CACHE is None else DBG_CACHE):
                r0 = (s * PAST + i * 128)
                S.dma("sync", lambda e, r0=r0: e.dma_start(out=kf2[:, :], in_=dap(X.c_dk, r0 * 1024, 1024, 128, 1024)),
                      writes=[k_f.b], dsem=k_f.ds)
                S.op("gpsimd", lambda e: e.tensor_copy(out=kb2[:, :], in_=kf2[:, :]), reads=[k_f.b], writes=[kd_b.b])
                S.dma("sync", lambda e, r0=r0: e.dma_start(out=v_f.t[:], in_=dap(X.c_dv, r0 * 1024, 1024, 128, 1024)),
                      writes=[v_f.b], dsem=v_f.ds)
                S.op("vector", lambda e: e.tensor_copy(out=v_b.t[:], in_=v_f.t[:]), reads=[v_f.b], writes=[v_b.b])
                kside(128, 1 + s, i * 128)

        tiles = token_tiles(K)
        xh = [K.x1[0], K.x1[1]]
        for it, tl in enumerate(tiles):
            n, seq, pos0, row0, oi = tl["n"], tl["seq"], tl["pos0"], tl["row0"], tl["oi"]
            xb = xt[it % 2]
            if it == 0:
                _ld_x(K, tiles, xt, xh, 0)
            if it + 1 < len(tiles):
                _ld_x(K, tiles, xt, xh, it + 1)
            rms_stats(K, xb.t[0:n, :], n, D, 1e-6, junk, ss, rs, [xb.b])
            S.op("vector", lambda e, xb=xb, n=n: e.scalar_tensor_tensor(out=xn.t[0:n, :], in0=xb.t[0:n, :],
                                                                        scalar=rs.t[0:n, 0:1], in1=gmix.t[0:n, :],
                                                                        op0=ALU.mult, op1=ALU.mult),
                 reads=[xb.b, rs.b, gmix.b], writes=[xn.b])
            transpose_blocks(K, xn, n, [(c * 128, 128) for c in range(8)], pt[0], hT, "scalar")
            for g6 in range(6):
                for c in range(8):
                    S.op("tensor", lambda e, g6=g6, c=c, n=n: e.matmul(
                        pb[g6].t[0:n, :], lhsT=hT.t[:, c, 0:n], rhs=w_c.t[:, c, g6 * 512:(g6 + 1) * 512],
                        start=(c == 0), stop=(c == 7)), reads=[hT.b] + WB, writes=[pb[g6].b])
            cos, sin = rope_tables(K, tl, n)
            cos8 = bass.AP(tensor=cos.tensor, offset=cos.offset, ap=[list(cos.ap[0]), [0, 8], [2, 8]])
            sin8 = bass.AP(tensor=sin.tensor, offset=sin.offset, ap=[list(sin.ap[0]), [0, 8], [2, 8]])
            for b2 in range(2):
                pq = pb[b2].t[:].rearrange("p (h c) -> p h c", c=64)
                rope(K, pq[0:n, :, 0:8], pq[0:n, :, 8:16], qd_b.t[0:n, b2 * 8:(b2 + 1) * 8, 0:8],
                     qd_b.t[0:n, b2 * 8:(b2 + 1) * 8, 8:16], cos8, sin8, t1.t[0:n, :, :], t2.t[0:n, :, :],
                     [pb[b2].b], [qd_b.b])
                S.op("vector", lambda e, b2=b2, pq=pq, n=n: e.tensor_copy(out=qd_b.t[0:n, b2 * 8:(b2 + 1) * 8, 16:64],
                                                                         in_=pq[0:n, :, 16:64]),
                     reads=[pb[b2].b], writes=[qd_b.b])
                pk = pb[2 + b2].t[:].rearrange("p (h c) -> p h c", c=64)
                rope(K, pk[0:n, :, 0:8], pk[0:n, :, 8:16], k_f.t[0:n, b2 * 8:(b2 + 1) * 8, 0:8],
                     k_f.t[0:n, b2 * 8:(b2 + 1) * 8, 8:16], cos8, sin8, t1.t[0:n, :, :], t2.t[0:n, :, :],
                     [pb[2 + b2].b], [k_f.b])
                S.op("vector", lambda e, b2=b2, pk=pk, n=n: e.tensor_copy(out=k_f.t[0:n, b2 * 8:(b2 + 1) * 8, 16:64],
                                                                         in_=pk[0:n, :, 16:64]),
                     reads=[pb[2 + b2].b], writes=[k_f.b])
                S.op("scalar", lambda e, b2=b2, n=n: e.activation(out=v_f.t[0:n, b2 * 512:(b2 + 1) * 512],
                                                                  in_=pb[4 + b2].t[0:n, :], func=AF.Copy),
                     reads=[pb[4 + b2].b], writes=[v_f.b])
            S.op("gpsimd", lambda e, n=n: e.tensor_copy(out=kb2[0:n, :], in_=kf2[0:n, :]), reads=[k_f.b],
                 writes=[kd_b.b])
            S.op("gpsimd", lambda e, n=n: e.tensor_copy(out=v_b.t[0:n, :], in_=v_f.t[0:n, :]), reads=[v_f.b],
                 writes=[v_b.b])
            store(K, dap(O.dk[oi], row0 * 1024, 1024, n, 1024), kf2[0:n, :], k_f)
            store(K, dap(O.dv[oi], row0 * 1024, 1024, n, 1024), v_f.t[0:n, :], v_f)
            sc = SC[seq]
            TqP = max(K.SEQS[seq]["Tq"], 32)
            qcol0 = pos0 - K.SEQS[seq]["past"]
            transpose_blocks(K, qd_b, n, [(b * 128, 128) for b in range(8)], pt[0], qdT_s, "vector")
            store(K, bass.AP(tensor=sc.qdT, offset=qcol0, ap=[[TqP, 128], [128 * TqP, 8], [1, n]]),
                  qdT_s.t[:, :, 0:n], qdT_s)
            kside(n, seq, pos0)
        S.barrier()
        S.emit()
```
